# Optimizing a Trainium2 kernel written in Bass

```python
import math
import jax, jax.numpy as jnp
from jax import lax
import numpy as np

D_MODEL = 2048
BATCH = 16
SEQ = 2048
DEPTH = 2

D_BRANCH = 1024
N_BRANCH = 3
POOL_GROUPS = 4
POOL_WINDOWS = (2, 4, 8, 16)
POOL_GROUP_DIM = D_BRANCH // POOL_GROUPS
SB_HEADS = 8
SB_HEAD_DIM = D_BRANCH // SB_HEADS
SB_BLOCK = 128
ML_HEADS = 4
ML_HEAD_DIM = D_BRANCH // ML_HEADS
ML_CHUNK = 64
ML_CONV = 4
D_FF = 5632
FFN_CONV = 3
ALPHA = (2.0 * DEPTH) ** 0.25
BETA = (8.0 * DEPTH) ** -0.25
LN_EPS = 1e-5

SPLIT_SIZES = (D_BRANCH, D_BRANCH, D_BRANCH, D_BRANCH, 2 * D_BRANCH, D_BRANCH, D_BRANCH,
               ML_HEADS, ML_HEADS, N_BRANCH * D_MODEL)
D_IN = sum(SPLIT_SIZES)
SPLIT_POINTS = tuple(int(p) for p in np.cumsum(SPLIT_SIZES)[:-1])

kernel_name = "hybrid_pool_stickbreak_mlstm_convffn_deepnorm"


def layer_norm(x, g, b):
    xf = x.astype(jnp.float32)
    mu = jnp.mean(xf, axis=-1, keepdims=True)
    var = jnp.mean(jnp.square(xf - mu), axis=-1, keepdims=True)
    y = (xf - mu) * lax.rsqrt(var + LN_EPS)
    return (y * g.astype(jnp.float32) + b.astype(jnp.float32)).astype(x.dtype)


def causal_dwconv(x, w):
    K = w.shape[0]
    S = x.shape[1]
    xp = jnp.pad(x, ((0, 0), (K - 1, 0), (0, 0)))
    out = xp[:, 0:S] * w[0]
    for k in range(1, K):
        out = out + xp[:, k:k + S] * w[k]
    return out


def pool_mixer(a, w_grp, scale):
    B, S, _ = a.shape
    a_g = a.reshape(B, S, POOL_GROUPS, POOL_GROUP_DIM).astype(jnp.float32)
    cs = jnp.cumsum(a_g, axis=1)
    t = jnp.arange(S)
    pooled = []
    for g, w in enumerate(POOL_WINDOWS):
        cs_g = cs[:, :, g]
        prev = jnp.pad(cs_g, ((0, 0), (w, 0), (0, 0)))[:, :S]
        cnt = jnp.minimum(t + 1, w).astype(jnp.float32)[None, :, None]
        pooled.append((cs_g - prev) / cnt)
    diff = (jnp.stack(pooled, axis=2) - a_g).astype(a.dtype)
    mixed = jnp.einsum('bsgc,gcd->bsgd', diff, w_grp)
    return mixed.reshape(B, S, D_BRANCH) * scale


def stick_breaking_attention(q, k, v):
    B, S, H, Dh = q.shape
    scale = Dh ** -0.5
    outs = []
    for blk in range(S // SB_BLOCK):
        start = blk * SB_BLOCK
        end = start + SB_BLOCK
        qb = q[:, start:end]
        kb = k[:, :end]
        vb = v[:, :end]
        z = jnp.einsum('bqhd,bkhd->bhqk', qb, kb).astype(jnp.float32) * scale
        t_idx = start + jnp.arange(SB_BLOCK)
        s_idx = jnp.arange(end)
        mask = s_idx[None, :] < t_idx[:, None]
        log_beta = jax.nn.log_sigmoid(z)
        log_keep = jnp.where(mask, jax.nn.log_sigmoid(-z), 0.0)
        after = lax.cumsum(log_keep, axis=3, reverse=True) - log_keep
        attn = jnp.where(mask, jnp.exp(log_beta + after), 0.0)
        outs.append(jnp.einsum('bhqk,bkhd->bqhd', attn.astype(v.dtype), vb))
    return jnp.concatenate(outs, axis=1)


def _to_chunks(x):
    B, S, H, D = x.shape
    return x.reshape(B, S // ML_CHUNK, ML_CHUNK, H, D).transpose(1, 0, 3, 2, 4)


def _gate_chunks(g):
    B, S, H = g.shape
    return g.reshape(B, S // ML_CHUNK, ML_CHUNK, H).transpose(1, 0, 3, 2)


def mlstm(q, k, v, i_pre, f_pre):
    B, S, H, Dh = q.shape
    f32 = jnp.float32
    qc = _to_chunks(q.astype(f32))
    kc = _to_chunks(k.astype(f32) * (Dh ** -0.5))
    vc = _to_chunks(v.astype(f32))
    ic = _gate_chunks(i_pre.astype(f32))
    fc = _gate_chunks(jax.nn.log_sigmoid(f_pre.astype(f32)))
    causal = jnp.tril(jnp.ones((ML_CHUNK, ML_CHUNK), dtype=bool))

    def step(carry, inp):
        C, n, m = carry
        qj, kj, vj, ij, fj = inp
        b = jnp.cumsum(fj, axis=-1)
        D = b[..., :, None] - b[..., None, :] + ij[..., None, :]
        D = jnp.where(causal, D, -jnp.inf)
        inter = b + m[..., None]
        m_t = jnp.maximum(inter, jnp.max(D, axis=-1))
        w_intra = jnp.exp(D - m_t[..., None])
        w_inter = jnp.exp(inter - m_t)
        qk = jnp.einsum('bhtd,bhsd->bhts', qj, kj) * w_intra
        num = (w_inter[..., None] * jnp.einsum('bhvk,bhtk->bhtv', C, qj)
               + jnp.einsum('bhts,bhsv->bhtv', qk, vj))
        den = w_inter * jnp.einsum('bhk,bhtk->bht', n, qj) + jnp.sum(qk, axis=-1)
        h = num / jnp.maximum(jnp.abs(den), jnp.exp(-m_t))[..., None]
        b_L = b[..., -1]
        dec = b_L[..., None] - b + ij
        m_new = jnp.maximum(b_L + m, jnp.max(dec, axis=-1))
        w_s = jnp.exp(dec - m_new[..., None])
        w_prev = jnp.exp(b_L + m - m_new)
        C_new = w_prev[..., None, None] * C + jnp.einsum('bhs,bhsv,bhsk->bhvk', w_s, vj, kj)
        n_new = w_prev[..., None] * n + jnp.einsum('bhs,bhsk->bhk', w_s, kj)
        return (C_new, n_new, m_new), h

    init = (jnp.zeros((B, H, Dh, Dh), f32), jnp.zeros((B, H, Dh), f32), jnp.zeros((B, H), f32))
    _, hs = lax.scan(step, init, (qc, kc, vc, ic, fc))
    return hs.transpose(1, 0, 3, 2, 4).reshape(B, S, H * Dh).astype(v.dtype)


def token_mixer(h, w_in_l, conv_ml_l, pool_w_l, pool_scale_l, ig_bias_l, fg_bias_l,
                w_branch_l, w_out_l):
    B, S, _ = h.shape
    proj = h @ w_in_l
    (a_pool, sb_q, sb_k, sb_v, ml_qk, ml_v, ml_o, ml_i, ml_f,
     gate_logits) = jnp.split(proj, SPLIT_POINTS, axis=-1)
    y_pool = pool_mixer(a_pool, pool_w_l, pool_scale_l)
    shp_sb = (B, S, SB_HEADS, SB_HEAD_DIM)
    y_sb = stick_breaking_attention(sb_q.reshape(shp_sb), sb_k.reshape(shp_sb),
                                    sb_v.reshape(shp_sb)).reshape(B, S, D_BRANCH)
    qk = jax.nn.silu(causal_dwconv(ml_qk, conv_ml_l))
    ml_q, ml_k = jnp.split(qk, 2, axis=-1)
    shp_ml = (B, S, ML_HEADS, ML_HEAD_DIM)
    y_ml = mlstm(ml_q.reshape(shp_ml), ml_k.reshape(shp_ml), ml_v.reshape(shp_ml),
                 ml_i + ig_bias_l, ml_f + fg_bias_l)
    y_ml = y_ml * jax.nn.sigmoid(ml_o)
    gates = jax.nn.sigmoid(gate_logits).reshape(B, S, N_BRANCH, D_MODEL)
    merged = gates[:, :, 0] * (y_pool @ w_branch_l[0])
    merged = merged + gates[:, :, 1] * (y_sb @ w_branch_l[1])
    merged = merged + gates[:, :, 2] * (y_ml @ w_branch_l[2])
    return merged @ w_out_l


def conv_ffn(h, w_up_l, conv_ff_l, w_down_l):
    up = causal_dwconv(h @ w_up_l, conv_ff_l)
    val, gate = jnp.split(up, 2, axis=-1)
    return (jax.nn.silu(gate) * val) @ w_down_l


def setup_inputs(seed: int = 0) -> dict:
    key = jax.random.key(seed)
    ks = jax.random.split(key, 17)

    def nrm(k, shape, s):
        return jax.random.normal(k, shape, jnp.float32) * s

    x = nrm(ks[0], (BATCH, SEQ, D_MODEL), 1.0)
    c = nrm(ks[1], (BATCH, D_MODEL), 1.0)
    w_ada = nrm(ks[2], (DEPTH, D_MODEL, 6 * D_MODEL), 0.1 * D_MODEL ** -0.5)
    b_ada = nrm(ks[3], (DEPTH, 6 * D_MODEL), 0.02)
    w_in = nrm(ks[4], (DEPTH, D_MODEL, D_IN), D_MODEL ** -0.5)
    conv_ml = nrm(ks[5], (DEPTH, ML_CONV, 2 * D_BRANCH), ML_CONV ** -0.5)
    pool_w = nrm(ks[6], (DEPTH, POOL_GROUPS, POOL_GROUP_DIM, POOL_GROUP_DIM), POOL_GROUP_DIM ** -0.5)
    pool_scale = 1.0 + nrm(ks[7], (DEPTH, D_BRANCH), 0.02)
    ig_bias = nrm(ks[8], (DEPTH, ML_HEADS), 0.1)
    fg_bias = jnp.linspace(3.0, 6.0, ML_HEADS, dtype=jnp.float32)[None, :] + nrm(ks[9], (DEPTH, ML_HEADS), 0.1)
    w_branch = nrm(ks[10], (DEPTH, N_BRANCH, D_BRANCH, D_MODEL), D_BRANCH ** -0.5)
    w_out = nrm(ks[11], (DEPTH, D_MODEL, D_MODEL), BETA * D_MODEL ** -0.5)
    w_up = nrm(ks[12], (DEPTH, D_MODEL, 2 * D_FF), D_MODEL ** -0.5)
    conv_ff = nrm(ks[13], (DEPTH, FFN_CONV, 2 * D_FF), FFN_CONV ** -0.5)
    w_down = nrm(ks[14], (DEPTH, D_FF, D_MODEL), BETA * D_FF ** -0.5)
    ln_g = 1.0 + nrm(ks[15], (DEPTH, 2, D_MODEL), 0.02)
    ln_b = nrm(ks[16], (DEPTH, 2, D_MODEL), 0.02)
    return {"x": x, "c": c, "w_ada": w_ada, "b_ada": b_ada, "w_in": w_in,
            "conv_ml": conv_ml, "pool_w": pool_w, "pool_scale": pool_scale,
            "ig_bias": ig_bias, "fg_bias": fg_bias, "w_branch": w_branch, "w_out": w_out,
            "w_up": w_up, "conv_ff": conv_ff, "w_down": w_down, "ln_g": ln_g, "ln_b": ln_b}


def reference(x, c, w_ada, b_ada, w_in, conv_ml, pool_w, pool_scale, ig_bias, fg_bias,
              w_branch, w_out, w_up, conv_ff, w_down, ln_g, ln_b):
    c_act = jax.nn.silu(c)
    for l in range(DEPTH):
        mod = c_act @ w_ada[l] + b_ada[l]
        sh1, sc1, g1, sh2, sc2, g2 = jnp.split(mod, 6, axis=-1)
        h = x * (1.0 + sc1[:, None]) + sh1[:, None]
        y = token_mixer(h, w_in[l], conv_ml[l], pool_w[l], pool_scale[l], ig_bias[l],
                        fg_bias[l], w_branch[l], w_out[l])
        x = layer_norm(ALPHA * x + (1.0 + g1[:, None]) * y, ln_g[l, 0], ln_b[l, 0])
        h = x * (1.0 + sc2[:, None]) + sh2[:, None]
        y = conv_ffn(h, w_up[l], conv_ff[l], w_down[l])
        x = layer_norm(ALPHA * x + (1.0 + g2[:, None]) * y, ln_g[l, 1], ln_b[l, 1])
    return x
```

```python
import numpy as np
from contextlib import ExitStack
import concourse.bass as bass
import concourse.mybir as mybir
from concourse.bass_utils import run_bass_kernel_spmd

F32 = mybir.dt.float32
BF16 = mybir.dt.bfloat16
AF = mybir.ActivationFunctionType
ALU = mybir.AluOpType

D = 2048
KC = 16
DB = 1024
DFF = 5632
DIN = 14344
T = 512
NSUB = 4
ALPHA = 4.0 ** 0.25
LN_EPS = 1e-5
POOL_W = (2, 4, 8, 16)
SB_SCALE = 128.0 ** -0.5
NCORES = 8


class Obj:
    __slots__ = ("name", "w", "r", "wd")
    fence = {}

    def __init__(self, name):
        self.name = name
        self.w = None
        self.wd = {}
        self.r = dict(Obj.fence)


class EngState:
    def __init__(self, name, h):
        self.name = name
        self.h = h
        self.sem = None
        self.count = 0
        self.seen = {}
        self.own = set()


class Sched:
    SEM_LIMIT = 30000
    NDS = 24

    def __init__(self, nc, es):
        self.nc = nc
        self.es = es
        self.nsem = 0
        self.eng = {}
        for name, h in (("pe", nc.tensor), ("act", nc.scalar), ("dve", nc.vector),
                        ("pool", nc.gpsimd), ("sp", nc.sync)):
            e = EngState(name, h)
            e.sem = self.new_sem("e" + name)
            e.own.add(id(e.sem))
            self.eng[name] = e
        self.dsem = {}
        self.dcnt = {}
        self.dval = {}
        for q in ("sp", "pool", "act"):
            self.dsem[q] = [self.new_sem("d" + q) for _ in range(self.NDS)]
            self.dval[q] = [0] * self.NDS
            self.dcnt[q] = 0
        self.ninst = 0

    def new_sem(self, tag):
        self.nsem += 1
        return self.es.enter_context(self.nc.semaphore(f"{tag}_{self.nsem}"))

    def _collect(self, e, reads, writes, is_dma):
        need = {}

        def add(ev, kind):
            if ev is None:
                return
            s, v = ev
            k = id(s)
            if k in e.own and not is_dma:
                if e.name == "pe":
                    return
            if e.seen.get(k, 0) >= v:
                return
            if k not in need or need[k][1] < v:
                need[k] = (s, v)

        for o in reads:
            add(o.w, "raw")
            for ev in o.wd.values():
                add(ev, "raw")
        for o in writes:
            add(o.w, "waw")
            for ev in o.wd.values():
                add(ev, "waw")
            for ev in o.r.values():
                add(ev, "war")
        return need

    def _wait(self, e, need):
        for k, (s, v) in need.items():
            e.h.wait_ge(s, v)
            e.seen[k] = v

    def _record(self, ev, reads, writes, is_dma=False):
        k = id(ev[0])
        for o in writes:
            if is_dma:
                o.wd[k] = ev
            else:
                o.w = ev
                o.wd = {}
            o.r = {}
        for o in reads:
            o.r[k] = ev

    def fence(self, site=-1):
        self.barrier()
        Obj.fence = {id(s): (s, v) for s, v in self.all_events()}

    def op(self, engname, fn, reads=(), writes=()):
        e = self.eng[engname]
        self._wait(e, self._collect(e, reads, writes, False))
        inst = fn(e.h)
        e.count += 1
        inst.then_inc(e.sem, 1)
        ev = (e.sem, e.count)
        self._record(ev, reads, writes)
        self.ninst += 1
        if e.count >= self.SEM_LIMIT:
            e.sem = self.new_sem("e" + engname)
            e.own.add(id(e.sem))
            e.count = 0
        return ev

    def dma(self, q, out, in_, reads=(), writes=(), **kw):
        e = self.eng[q]
        i = self.dcnt[q] % self.NDS
        self.dcnt[q] += 1
        s = self.dsem[q][i]
        need = self._collect(e, reads, writes, True)
        pv = self.dval[q][i]
        if pv > 0 and e.seen.get(id(s), 0) < pv:
            need[id(s)] = (s, pv)
        self._wait(e, need)
        e.h.dma_start(out=out, in_=in_, **kw).then_inc(s, 16)
        self.dval[q][i] = pv + 16
        ev = (s, pv + 16)
        self._record(ev, reads, writes, True)
        self.ninst += 1
        return ev

    def all_events(self):
        evs = []
        for e in self.eng.values():
            if e.count > 0:
                evs.append((e.sem, e.count))
        for q in self.dsem:
            for i, s in enumerate(self.dsem[q]):
                if self.dval[q][i] > 0:
                    evs.append((s, self.dval[q][i]))
        return evs

    def barrier(self, engines=("pe", "act", "dve", "pool", "sp")):
        evs = self.all_events()
        for n in engines:
            e = self.eng[n]
            for s, v in evs:
                k = id(s)
                if e.seen.get(k, 0) >= v:
                    continue
                if k in e.own and (s is e.sem) and n == "pe":
                    pass
                e.h.wait_ge(s, v)
                e.seen[k] = v


def run_rr(gens):
    active = list(gens)
    while active:
        for g in list(active):
            try:
                next(g)
            except StopIteration:
                active.remove(g)


class Rot:
    def __init__(self, aps, name):
        self.aps = aps
        self.objs = [Obj(f"{name}{i}") for i in range(len(aps))]
        self.i = 0

    def next(self):
        i = self.i % len(self.aps)
        self.i += 1
        return self.aps[i], self.objs[i]


def build_program(NSEQ, S, L, dbg=False):
    Obj.fence = {}
    nc = bass.Bass("TRN2", target_bir_lowering=False)
    es = ExitStack()
    NT = S // T
    NKT = S // 128

    def din(name, shape):
        return nc.dram_tensor(name, list(shape), F32, kind="ExternalInput").ap()

    x_d = din("x", [NSEQ * S, D])
    c_d = din("c", [NSEQ * KC, 128])
    w_ada = din("w_ada", [L, D, 6 * D])
    b_ada = din("b_ada", [L * 96, 128])
    w_in = din("w_in", [L, D, DIN])
    conv_ml = din("conv_ml", [L * 4 * 16, 128])
    pool_w = din("pool_w", [L, 4, 256, 256])
    pool_scale = din("pool_scale", [L * 8, 128])
    ig_bias = din("ig_bias", [L, 4, 1])
    fg_bias = din("fg_bias", [L, 4, 1])
    w_branch = din("w_branch", [L, 3, DB, D])
    w_out = din("w_out", [L, D, D])
    w_up = din("w_up", [L, D, 2 * DFF])
    conv_ff = din("conv_ff", [L * 3 * 88, 128])
    w_down = din("w_down", [L, DFF, D])
    ln_g = din("ln_g", [L * 2 * 16, 128])
    ln_b = din("ln_b", [L * 2 * 16, 128])
    cst_d = din("cst", [128, 9, 128])
    sel_d = din("sel", [4, 4 * 128])
    rcnt_d = din("rcnt", [128, 4 * 16])
    out_d = nc.dram_tensor("out", [NSEQ * S, D], F32, kind="ExternalOutput").ap()

    def dscr(name, shape, dt):
        return nc.dram_tensor(name, list(shape), dt, kind="Internal").ap()

    wb_in = dscr("wb_in", [L, 113, 128, 16, 128], BF16)
    wb_br = dscr("wb_br", [L, 3, 16, 128, 8, 128], BF16)
    wb_out = dscr("wb_out", [L, 16, 128, 16, 128], BF16)
    wb_up = dscr("wb_up", [L, 88, 128, 16, 128], BF16)
    wb_dn = dscr("wb_dn", [L, 16, 4, 128, 11, 128], BF16)
    xs_d = dscr("xs", [2, NSEQ, 128, 16, S], F32)
    x1_d = dscr("x1s", [128, 16, T], F32)
    k_d = dscr("kscr", [8, 128, S], BF16)
    v_d = dscr("vscr", [NKT, 128, DB], BF16)
    xs_obj = [[[Obj(f"xs{j}_{b}_{t}") for t in range(NT)] for b in range(NSEQ)] for j in range(2)]
    x1_obj = Obj("x1s")
    k_obj = Obj("kscr")
    v_obj = Obj("vscr")
    wb_obj = Obj("wb")

    dbg_out = {}
    if dbg:
        for nm, shp in (("d_h", [128, 16, T]), ("d_ypool", [128, 8, T]), ("d_ysb", [128, 8, T]),
                        ("d_yml", [128, 8, T]), ("d_merged", [128, 16, T]), ("d_x1", [128, 16, T]),
                        ("d_mod", [128, L * NSEQ * 96])):
            dbg_out[nm] = nc.dram_tensor(nm, shp, F32, kind="ExternalOutput").ap()

    Obj.fence = {}
    S_ = Sched(nc, es)
    op = S_.op
    dma = S_.dma

    sbn = [0]

    def sb(name, shape, dt, stack=es):
        sbn[0] += 1
        return stack.enter_context(nc.sbuf_tensor(f"{name}_{sbn[0]}", list(shape), dt))

    psb = [es.enter_context(nc.psum_tensor(f"ps{i}", [128, 512], F32)) for i in range(8)]
    pso = [Obj(f"ps{i}") for i in range(8)]

    cst_f = sb("cst_f", [128, 9, 128], F32)
    cst_b = sb("cst_b", [128, 9, 128], BF16)
    IDF, UIN, LST, SMK, MLE, ONE, OND = 0, 1, 2, 3, 4, 5, 6
    sel_f = sb("sel_f", [4, 4 * 128], F32)
    rcnt_f = sb("rcnt_f", [128, 4 * 16], F32)
    ones_row = sb("ones_row", [4, T], F32)
    zero_row = sb("zero_row", [4, T], F32)
    modt = sb("modt", [128, L * NSEQ * 96], F32)
    a2t = sb("a2t", [128, 2 * 16], F32)
    cml = sb("cml", [128, L * 64], F32)
    cff = sb("cff", [128, L * 264], F32)
    psc = sb("psc", [128, L * 8], F32)
    lng = sb("lng", [128, L * 32], F32)
    lnb = sb("lnb", [128, L * 32], F32)
    igb = sb("igb", [4, L], F32)
    fgb = sb("fgb", [4, L], F32)
    poolw_b = sb("poolw_b", [128, 4, 2, 256], BF16)
    hT = sb("hT", [128, 16, T], BF16)
    NSLOT = 6
    wring = sb("wring", [128, NSLOT, 16, 128], BF16)
    cstate = sb("cstate", [128, 4, 2, 384], F32)
    cbf = sb("cbf", [128, 4, 2, 384], BF16)
    pohalo = sb("pohalo", [128, 8, 15], F32)
    mlhalo = sb("mlhalo", [128, 16, 3], F32)
    ffhalo = sb("ffhalo", [128, 88, 2], F32)
    bcar = sb("bcar", [4, 1], F32)
    mcar = sb("mcar", [4, 1], F32)

    o_cst = Obj("cst")
    o_mod = Obj("mod")
    o_hT = Obj("hT")
    o_cstate = [Obj(f"cst{h}") for h in range(4)]
    o_cbf = [Obj(f"cbf{h}") for h in range(4)]
    o_pohalo = Obj("pohalo")
    o_mlhalo = Obj("mlhalo")
    o_ffhalo = Obj("ffhalo")
    o_car = Obj("car")
    o_poolw = Obj("poolw")
    o_a2 = Obj("a2")
    wslot_obj = [Obj(f"wslot{i}") for i in range(NSLOT)]

    def cb(i):
        return cst_b[:, i, :]

    def cf(i):
        return cst_f[:, i, :]

    def mcol(l, b, seg, ch):
        i = ((l * NSEQ + b) * 6 + seg) * 16 + ch
        return modt[:, i:i + 1]

    def act(out, in_, func, reads, writes, bias=None, scale=None):
        kw = {}
        if bias is not None:
            kw["bias"] = bias
        if scale is not None:
            kw["scale"] = scale
        return op("act", lambda h: h.activation(out=out, in_=in_, func=func, **kw), reads, writes)

    def tt(eng, out, in0, in1, alu, reads, writes):
        return op(eng, lambda h: h.tensor_tensor(out=out, in0=in0, in1=in1, op=alu), reads, writes)

    def ts(eng, out, in0, s1, op0, reads, writes, s2=None, op1=None):
        if op1 is None:
            return op(eng, lambda h: h.tensor_scalar(out=out, in0=in0, scalar1=s1, scalar2=None, op0=op0),
                      reads, writes)
        return op(eng, lambda h: h.tensor_scalar(out=out, in0=in0, scalar1=s1, scalar2=s2, op0=op0, op1=op1),
                  reads, writes)

    def stt(out, in0, scalar, in1, op0, op1, reads, writes):
        return op("dve", lambda h: h.scalar_tensor_tensor(out=out, in0=in0, scalar=scalar, in1=in1,
                                                          op0=op0, op1=op1), reads, writes)

    def copy(eng, out, in_, reads, writes):
        if eng == "act":
            return act(out, in_, AF.Copy, reads, writes)
        return op(eng, lambda h: h.tensor_copy(out=out, in_=in_), reads, writes)

    def memset(eng, ap, val, writes):
        return op(eng, lambda h: h.memset(ap, val), (), writes)

    def mm(out, lhsT, rhs, start, stop, reads, writes):
        return op("pe", lambda h: h.matmul(out, lhsT, rhs, start=start, stop=stop), reads, writes)

    def tr(out, in_, ident, reads, writes):
        return op("pe", lambda h: h.transpose(out, in_, ident), reads, writes)

    dma("sp", cst_f[:], cst_d[:, :, :], (), (o_cst,))
    dma("sp", sel_f[:], sel_d[:, :], (), (o_cst,))
    dma("sp", rcnt_f[:], rcnt_d[:, :], (), (o_cst,))
    for l in range(L):
        dma("sp", igb[:, l:l + 1], ig_bias[l], (), (o_cst,))
        dma("sp", fgb[:, l:l + 1], fg_bias[l], (), (o_cst,))
    copy("dve", cst_b[:], cst_f[:], (o_cst,), (o_cst,))
    memset("dve", ones_row[:], 1.0, (o_cst,))
    memset("dve", zero_row[:], 0.0, (o_cst,))

    with ExitStack() as ps_:
        rows = sb("rows", [128, 128], F32, ps_)
        o_rows = Obj("rows")
        cT = sb("cT", [128, 16, NSEQ], F32, ps_)
        bada = sb("bada", [128, L * 96], F32, ps_)
        o_small = Obj("small")

        def vec_cols(src_ap, nrows, dst_ap, post=None):
            dma("sp", rows[0:nrows, :], src_ap, (), (o_rows,))
            tr(psb[0][:, 0:nrows], rows[0:nrows, :], cf(IDF)[0:nrows, 0:nrows], (o_rows, o_cst), (pso[0],))
            if post is None:
                copy("dve", dst_ap, psb[0][:, 0:nrows], (pso[0],), (o_small,))
            else:
                post(dst_ap, psb[0][:, 0:nrows])

        def chunks(n, m=128):
            i = 0
            while i < n:
                yield i, min(m, n - i)
                i += m

        for i, n in chunks(L * 96):
            vec_cols(b_ada[i:i + n, :], n, bada[:, i:i + n])
        for i, n in chunks(L * 64):
            vec_cols(conv_ml[i:i + n, :], n, cml[:, i:i + n])
        for i, n in chunks(L * 264):
            vec_cols(conv_ff[i:i + n, :], n, cff[:, i:i + n])
        vec_cols(pool_scale[:, :], L * 8, psc[:, :])
        vec_cols(ln_g[:, :], L * 32, lng[:, :])
        vec_cols(ln_b[:, :], L * 32, lnb[:, :])

        def post_c(dst, src):
            sg = sb("csig", [128, NSEQ * 16], F32, ps_)
            act(sg[:], src, AF.Sigmoid, (pso[0],), (o_small,))
            tt("dve", dst, sg[:].rearrange("p (b k) -> p k b", b=NSEQ), src.rearrange("p (b k) -> p k b", b=NSEQ),
               ALU.mult, (pso[0], o_small), (o_small,))

        vec_cols(c_d[:, :], NSEQ * 16, cT[:], post=post_c)

        wada_s = sb("wada_s", [128, 3, 16, 128], F32, ps_)
        wada_rot = Rot([wada_s[:, i] for i in range(3)], "wada")
        def mod_gen():
            bank = 0
            for l in range(L):
                for j in range(96):
                    slot, so = wada_rot.next()
                    dma("sp", slot, w_ada[l, :, j * 128:(j + 1) * 128].rearrange("(k p) c -> p k c", p=128), (), (so,))
                    pb, po = psb[bank % 4], pso[bank % 4]
                    bank += 1
                    for kc in range(16):
                        mm(pb[:, 0:NSEQ], slot[:, kc, :], cT[:, kc, :], kc == 0, kc == 15, (so, o_small), (po,))
                    seg = j // 16
                    ch = j % 16
                    for b in range(NSEQ):
                        addc = 1.0 if seg in (1, 2, 4, 5) else 0.0
                        ts("dve", mcol(l, b, seg, ch), pb[:, b:b + 1], bada[:, l * 96 + j:l * 96 + j + 1], ALU.add,
                           (po, o_small), (o_mod,), s2=addc, op1=ALU.add)
                    if j % 3 == 2:
                        yield
            if dbg:
                dma("sp", dbg_out["d_mod"][:, :], modt[:], (o_mod,), ())

        xin_t = sb("xin_t", [128, 2, D], F32, ps_)
        xin_rot = Rot([xin_t[:, i] for i in range(2)], "xin_t")
        xo_t = sb("xo_t", [128, 2, 16, 128], F32, ps_)
        xo_rot = Rot([xo_t[:, i] for i in range(2)], "xo_t")

        def xt_gen():
            for b in range(NSEQ):
                for r in range(NKT):
                    xi, xio = xin_rot.next()
                    xo, xoo = xo_rot.next()
                    dma("pool", xi, x_d[b * S + r * 128:b * S + (r + 1) * 128, :], (), (xio,))
                    for g4 in range(4):
                        pb, po = psb[4 + g4], pso[4 + g4]
                        for j in range(4):
                            ch = g4 * 4 + j
                            tr(pb[:, j * 128:(j + 1) * 128], xi[:, ch * 128:(ch + 1) * 128], cf(IDF), (xio, o_cst), (po,))
                        copy("act", xo[:, g4 * 4:(g4 + 1) * 4, :],
                             pb[:, :].rearrange("p (a b) -> p a b", a=4), (po,), (xoo,))
                    dma("pool", xs_d[0, b, :, :, r * 128:(r + 1) * 128], xo, (xoo,), (xs_obj[0][b][(r * 128) // T],))
                    yield

        run_rr([mod_gen(), xt_gen()])
        S_.barrier()

    def f32src(l, key):
        if key[0] == "in":
            bb = key[1]
            if bb == 112:
                return w_in[l, :, 8192:8200], 8
            return w_in[l, :, bb * 128:(bb + 1) * 128], 128
        if key[0] == "gate":
            c0 = 8200 + (key[1] * 16 + key[2]) * 128
            return w_in[l, :, c0:c0 + 128], 128
        if key[0] == "br":
            return w_branch[l, key[1], :, key[2] * 128:(key[2] + 1) * 128], 128
        if key[0] == "out":
            return w_out[l, :, key[1] * 128:(key[1] + 1) * 128], 128
        if key[0] == "up":
            return w_up[l, :, key[1] * 128:(key[1] + 1) * 128], 128
        if key[0] == "dn":
            return w_down[l, key[2] * 1408:(key[2] + 1) * 1408, key[1] * 128:(key[1] + 1) * 128], 128
        raise KeyError(key)

    def tile_plan(l, jit):
        plan = []
        for b in range(64):
            plan.append((("in", b), wb_in[l, b], 16))
        plan.append((("in", 112), wb_in[l, 112], 16))
        for d in range(16):
            for br in range(3):
                plan.append((("gate", br, d), wb_in[l, 64 + br * 16 + d], 16))
                plan.append((("br", br, d), wb_br[l, br, d], 8))
        for d in range(16):
            plan.append((("out", d), wb_out[l, d], 16))
        for p in range(44):
            plan.append((("up", p), wb_up[l, p], 16))
            plan.append((("up", 44 + p), wb_up[l, 44 + p], 16))
        for d in range(16):
            for pc in range(4):
                plan.append((("dn", d, pc), wb_dn[l, d, pc], 11))
        out = []
        for key, dst, kcn in plan:
            if jit:
                src, ncols = f32src(l, key)
                out.append((key, dst, kcn, src.rearrange("(k p) c -> p k c", p=128), ncols, l))
            else:
                out.append((key, dst, kcn, None, 128, l))
        return out

    gplan = []
    for b in range(NSEQ):
        for l in range(L):
            for ti in range(NT):
                gplan.extend(tile_plan(l, b == 0 and ti == 0))
    wst = {"issued": 0, "ptr": 0, "cast": 0, "ce": 0}
    NSTG = 2
    jstg = sb("jstg", [128, NSTG, 16, 128], F32)
    jstg_obj = [Obj(f"jstg{i}") for i in range(NSTG)]
    wbl_obj = [Obj(f"wb{l}") for l in range(L)]

    def w_issue_upto(n):
        while wst["issued"] < min(n, len(gplan)):
            i = wst["issued"]
            key, dst, kcn, src32, ncols, l_ = gplan[i]
            sl = i % NSLOT
            if src32 is None:
                dma("sp", wring[:, sl, 0:kcn, :], dst, (wbl_obj[l_],), (wslot_obj[sl],))
            else:
                if i >= wst["cast"] + NSTG:
                    return
                st = i % NSTG
                dma("sp", jstg[:, st, 0:kcn, 0:ncols], src32, (), (jstg_obj[st],))
            wst["issued"] += 1

    def w_cast_upto(n):
        while wst["cast"] < min(n, len(gplan), wst["issued"]):
            i = wst["cast"]
            key, dst, kcn, src32, ncols, l_ = gplan[i]
            if src32 is not None:
                sl = i % NSLOT
                st = i % NSTG
                eng = "act" if wst["ce"] % 2 == 0 else "dve"
                wst["ce"] += 1
                copy(eng, wring[:, sl, 0:kcn, 0:ncols], jstg[:, st, 0:kcn, 0:ncols], (jstg_obj[st],), (wslot_obj[sl],))
                if ncols == 128:
                    dma("pool", dst, wring[:, sl, 0:kcn, :], (wslot_obj[sl],), (wbl_obj[l_],))
                else:
                    dma("pool", dst[:, :, 0:ncols], wring[:, sl, 0:kcn, 0:ncols], (wslot_obj[sl],), (wbl_obj[l_],))
            wst["cast"] += 1

    def w_take(key):
        i = wst["ptr"]
        assert gplan[i][0] == key, (gplan[i][0], key)
        for _ in range(NSLOT + 2):
            w_cast_upto(i + 1)
            w_issue_upto(i + NSLOT - 1)
        w_cast_upto(i + 1)
        assert wst["issued"] > i and wst["cast"] > i
        wst["ptr"] += 1
        sl = i % NSLOT
        return wring[:, sl], wslot_obj[sl]

    def w_after():
        w_cast_upto(wst["ptr"] + NSLOT - 3)
        w_issue_upto(wst["ptr"] + NSLOT - 1)
        w_cast_upto(wst["ptr"] + NSLOT - 3)

    def proj(key, kcn, rhs_fn, rhs_objs, pb, po, m=128):
        wsl, wo = w_take(key)
        for kc in range(kcn):
            mm(pb[0:m, 0:T], wsl[:, kc, 0:m], rhs_fn(kc), kc == 0, kc == kcn - 1, (wo,) + tuple(rhs_objs), (po,))

    for b in range(NSEQ):
        for l in range(L):
            last_layer = (l == L - 1)
            with ExitStack() as st_:
                pwf = sb("pwf", [128, 4, 2, 256], F32, st_)
                o_pwf = Obj("pwf")
                dma("sp", pwf[:], pool_w[l].rearrange("g (k p) c -> p g k c", p=128), (), (o_pwf,))
                copy("dve", poolw_b[:], pwf[:], (o_pwf,), (o_poolw,))
                for ch in range(16):
                    g1c = lng[:, l * 32 + ch:l * 32 + ch + 1]
                    b1c = lnb[:, l * 32 + ch:l * 32 + ch + 1]
                    tt("dve", a2t[:, ch:ch + 1], g1c, mcol(l, b, 4, ch), ALU.mult, (o_mod,), (o_a2,))
                    stt(a2t[:, 16 + ch:17 + ch], b1c, mcol(l, b, 4, ch), mcol(l, b, 3, ch), ALU.mult, ALU.add,
                        (o_mod,), (o_a2,))
                S_.fence(0)
            memset("dve", cstate[:], 0.0, tuple(o_cstate))
            memset("dve", cbf[:], 0.0, tuple(o_cbf))
            memset("dve", pohalo[:], 0.0, (o_pohalo,))
            memset("dve", mlhalo[:], 0.0, (o_mlhalo,))
            memset("dve", ffhalo[:], 0.0, (o_ffhalo,))
            memset("dve", bcar[:], 0.0, (o_car,))
            memset("dve", mcar[:], 0.0, (o_car,))

            for ti in range(NT):
                t0 = ti * T
                first_tile = (ti == 0)
                src_j = l % 2
                dst_j = (l + 1) % 2
                with ExitStack() as pa_:
                    xin = sb("xin", [128, 16, T], F32, pa_)
                    o_xin = Obj("xin")
                    for q4 in range(4):
                        dma("pool", xin[:, q4 * 4:(q4 + 1) * 4, :], xs_d[src_j, b, :, q4 * 4:(q4 + 1) * 4, t0:t0 + T],
                            (xs_obj[src_j][b][ti],), (o_xin,))
                    for ch in range(16):
                        act(hT[:, ch, :], xin[:, ch, :], AF.Identity, (o_xin, o_mod), (o_hT,),
                            bias=mcol(l, b, 0, ch), scale=mcol(l, b, 1, ch))
                    if dbg and ti == NT - 1:
                        dbgt = sb("dbgt", [128, 16, T], F32, pa_)
                        o_dbgt = Obj("dbgt")
                        copy("dve", dbgt[:], hT[:], (o_hT,), (o_dbgt,))
                        dma("pool", dbg_out["d_h"][:, :, :], dbgt[:], (o_dbgt,), ())
                    S_.fence(1)

                def h_rhs(kc):
                    return hT[:, kc, :]

                with ExitStack() as pm_:
                    ybr = sb("ybr", [128, 3, 8, T], BF16, pm_)
                    o_ybr = [Obj(f"ybr{i}") for i in range(3)]
                    with ExitStack() as pp_:
                        apool = sb("apool", [128, 8, 15 + T], F32, pp_)
                        o_apool = [Obj(f"apool{g}") for g in range(4)]
                        ptmp = sb("ptmp", [128, 2, 2, 15 + T], F32, pp_)
                        o_ptmp = [Obj("ptmp0"), Obj("ptmp1")]
                        dTt = sb("dTt", [128, 8, T], BF16, pp_)
                        o_dT = [Obj(f"dT{g}") for g in range(4)]
                        for g in range(4):
                            copy("pool", apool[:, 2 * g:2 * g + 2, 0:15], pohalo[:, 2 * g:2 * g + 2, :], (o_pohalo,),
                                 (o_apool[g],))
                        for cch in range(8):
                            pb, po = psb[cch % 2], pso[cch % 2]
                            proj(("in", cch), 16, h_rhs, (o_hT,), pb, po)
                            copy("act", apool[:, cch, 15:15 + T], pb[:, 0:T], (po,), (o_apool[cch // 2],))
                            w_after()
                        for g in range(4):
                            cur = apool[:, 2 * g:2 * g + 2, :]
                            cur_o = o_apool[g]
                            W = 15 + T
                            for k in range(g + 1):
                                sh = 1 << k
                                dst = ptmp[:, k % 2]
                                dst_o = o_ptmp[k % 2]
                                eng = "pool" if g % 2 == 0 else "dve"
                                lo = 2 * sh - 1
                                tt(eng, dst[:, :, lo:W], cur[:, :, lo:W], cur[:, :, lo - sh:W - sh], ALU.add,
                                   (cur_o,), (dst_o,))
                                cur, cur_o = dst, dst_o
                            if first_tile:
                                rc = rcnt_f[:, g * 16:(g + 1) * 16]
                                for j2 in range(2):
                                    tt("dve", cur[:, j2, 15:31], cur[:, j2, 15:31], rc, ALU.mult, (cur_o, o_cst),
                                       (cur_o,))
                            stt(dTt[:, 2 * g:2 * g + 2, :], cur[:, :, 15:15 + T], 1.0 / POOL_W[g],
                                apool[:, 2 * g:2 * g + 2, 15:15 + T], ALU.mult, ALU.subtract,
                                (cur_o, o_apool[g]), (o_dT[g],))
                            for m in range(2):
                                pb, po = psb[2 + (2 * g + m) % 2], pso[2 + (2 * g + m) % 2]
                                for kc in range(2):
                                    mm(pb[:, 0:T], poolw_b[:, g, kc, m * 128:(m + 1) * 128], dTt[:, 2 * g + kc, :],
                                       kc == 0, kc == 1, (o_poolw, o_dT[g]), (po,))
                                ch = 2 * g + m
                                act(ybr[:, 0, ch, :], pb[:, 0:T], AF.Copy, (po,), (o_ybr[0],),
                                    scale=psc[:, l * 8 + ch:l * 8 + ch + 1])
                        for g in range(4):
                            copy("pool", pohalo[:, 2 * g:2 * g + 2, :], apool[:, 2 * g:2 * g + 2, T:T + 15],
                                 (o_apool[g],), (o_pohalo,))
                        S_.fence(2)

                    with ExitStack() as pa2_:
                        qT = sb("qT", [128, 8, T], BF16, pa2_)
                        o_qT = Obj("qT")
                        kst = sb("kst", [128, 8, T], BF16, pa2_)
                        o_kst = Obj("kst")
                        vst = sb("vst", [128, 2, T], BF16, pa2_)
                        vst_rot = Rot([vst[:, i] for i in range(2)], "vst")
                        vtok = sb("vtok", [128, NSUB, DB], BF16, pa2_)
                        o_vtok = Obj("vtok")
                        kbuf = sb("kbuf", [128, 3, S], BF16, pa2_)
                        kbuf_rot = Rot([kbuf[:, i] for i in range(3)], "kbuf")
                        vbuf = sb("vbuf", [128, 3, NKT, 128], BF16, pa2_)
                        vbuf_rot = Rot([vbuf[:, i] for i in range(3)], "vbuf")
                        G_AT = 2
                        ebuf = sb("ebuf", [128, G_AT, 2, T], F32, pa2_)
                        e_rot = [Rot([ebuf[:, g, i] for i in range(2)], f"ebuf{g}") for g in range(G_AT)]
                        spbuf = sb("spbuf", [128, G_AT, 2, T], BF16, pa2_)
                        sp_rot = [Rot([spbuf[:, g, i] for i in range(2)], f"spbuf{g}") for g in range(G_AT)]
                        tbuf = sb("tbuf", [128, G_AT, 2, T], F32, pa2_)
                        t_rot = [Rot([tbuf[:, g, i] for i in range(2)], f"tbuf{g}") for g in range(G_AT)]
                        abuf = sb("abuf", [128, G_AT, 2, T], BF16, pa2_)
                        a_rot = [Rot([abuf[:, g, i] for i in range(2)], f"abuf{g}") for g in range(G_AT)]
                        for hh in range(8):
                            pb, po = psb[hh % 2], pso[hh % 2]
                            proj(("in", 8 + hh), 16, h_rhs, (o_hT,), pb, po)
                            copy("act", qT[:, hh, :], pb[:, 0:T], (po,), (o_qT,))
                            w_after()
                        for hh in range(8):
                            pb, po = psb[hh % 2], pso[hh % 2]
                            proj(("in", 16 + hh), 16, h_rhs, (o_hT,), pb, po)
                            copy("act", kst[:, hh, :], pb[:, 0:T], (po,), (o_kst,))
                            w_after()
                        dma("pool", k_d[:, :, t0:t0 + T].rearrange("h p t -> p h t"), kst[:], (o_kst,), (k_obj,))
                        psbf = psb[7].bitcast(BF16)
                        for hh in range(8):
                            pb, po = psb[hh % 2], pso[hh % 2]
                            proj(("in", 24 + hh), 16, h_rhs, (o_hT,), pb, po)
                            vs, vso = vst_rot.next()
                            copy("dve", vs, pb[:, 0:T], (po,), (vso,))
                            w_after()
                            for sub in range(NSUB):
                                tr(psbf[:, sub * 128:(sub + 1) * 128], vs[:, sub * 128:(sub + 1) * 128], cb(IDF),
                                   (vso, o_cst), (pso[7],))
                            copy("act", vtok[:, :, hh * 128:(hh + 1) * 128],
                                 psbf[:, 0:512].rearrange("p (s c) -> p s c", s=NSUB), (pso[7],), (o_vtok,))
                        dma("pool", v_d[ti * NSUB:(ti + 1) * NSUB].rearrange("s p c -> p s c"), vtok[:], (o_vtok,),
                            (v_obj,))
                        nk = (ti + 1) * NSUB

                        def attn_gen(hh, g):
                            kb, kbo = kbuf_rot.next()
                            vb, vbo = vbuf_rot.next()
                            dma("pool", kb[:, 0:nk * 128], k_d[hh, :, 0:nk * 128], (k_obj,), (kbo,))
                            dma("pool", vb[:, 0:nk, :], v_d[0:nk, :, hh * 128:(hh + 1) * 128].rearrange("s p c -> p s c"),
                                (v_obj,), (vbo,))
                            pZ, oZ = psb[g], pso[g]
                            pR, oR = psb[2 + g], pso[2 + g]
                            pO, oO = psb[4 + g], pso[4 + g]
                            js = list(range(nk - 1, -1, -1))

                            def qlo_of(j):
                                return max(0, j - ti * NSUB) * 128

                            def zmm(j):
                                ql = qlo_of(j)
                                mm(pZ[:, ql:T], kb[:, j * 128:(j + 1) * 128], qT[:, hh, ql:T], True, True,
                                   (kbo, o_qT), (oZ,))

                            zmm(js[0])
                            yield
                            for idx, j in enumerate(js):
                                dloc = j - ti * NSUB
                                qlo = qlo_of(j)
                                firstj = (idx == 0)
                                ee, eo = e_rot[g].next()
                                sp, spo = sp_rot[g].next()
                                tb, tbo = t_rot[g].next()
                                ab, abo = a_rot[g].next()
                                act(ee[:, qlo:T], pZ[:, qlo:T], AF.Exp, (oZ,), (eo,), scale=SB_SCALE)
                                act(sp[:, qlo:T], ee[:, qlo:T], AF.Ln, (eo,), (spo,), bias=1.0)
                                if dloc >= 0:
                                    tt("pool", sp[:, qlo:qlo + 128], sp[:, qlo:qlo + 128], cb(SMK), ALU.mult,
                                       (spo, o_cst), (spo,))
                                yield
                                mm(pR[:, qlo:T], cb(UIN), sp[:, qlo:T], firstj, False, (spo, o_cst), (oR,))
                                if idx + 1 < len(js):
                                    zmm(js[idx + 1])
                                yield
                                act(tb[:, qlo:T], pR[:, qlo:T], AF.Exp, (oR,), (tbo,), scale=-1.0)
                                yield
                                mm(pR[:, qlo:T], cb(LST), sp[:, qlo:T], False, j == 0, (spo, o_cst), (oR,))
                                tt("dve", ab[:, qlo:T], ee[:, qlo:T], tb[:, qlo:T], ALU.mult, (eo, tbo), (abo,))
                                if dloc >= 0:
                                    tt("pool", ab[:, qlo:qlo + 128], ab[:, qlo:qlo + 128], cb(SMK), ALU.mult,
                                       (abo, o_cst), (abo,))
                                mm(pO[:, qlo:T], vb[:, j, :], ab[:, qlo:T], firstj, j == 0, (vbo, abo), (oO,))
                                yield
                            copy("dve", ybr[:, 1, hh, :], pO[:, 0:T], (oO,), (o_ybr[1],))

                        for h0 in range(0, 8, G_AT):
                            run_rr([attn_gen(h0 + g, g) for g in range(G_AT)])
                        S_.fence(3)

                    with ExitStack() as pl_:
                        qm = sb("qm", [128, 16, T], BF16, pl_)
                        o_qm = Obj("qm")
                        vext = sb("vext", [128, NSUB, 4, 384], BF16, pl_)
                        o_vext = Obj("vext")
                        sgo = sb("sgo", [128, 8, T], BF16, pl_)
                        o_sgo = Obj("sgo")
                        rws = sb("rws", [4, 5, T], F32, pl_)
                        o_rws = Obj("rws")
                        gcol = sb("gcol", [128, 16], F32, pl_)
                        o_gcol = Obj("gcol")
                        pc_ = ExitStack()
                        cst_s = sb("cst_s", [128, 2, 3 + T], F32, pc_)
                        cst_rot = Rot([cst_s[:, i] for i in range(2)], "cst_s")
                        acc_s = sb("acc_s", [128, 2, T], F32, pc_)
                        acc_rot = Rot([acc_s[:, i] for i in range(2)], "acc_s")
                        sig_s = sb("sig_s", [128, 2, T], F32, pc_)
                        sig_rot = Rot([sig_s[:, i] for i in range(2)], "sig_s")
                        vst2 = sb("vst2", [128, 2, T], BF16, pc_)
                        vst2_rot = Rot([vst2[:, i] for i in range(2)], "vst2")
                        memset("pool", vext[:, :, :, 256:384], 1.0, (o_vext,))
                        for blk in range(16):
                            pb, po = psb[blk % 2], pso[blk % 2]
                            proj(("in", 32 + blk), 16, h_rhs, (o_hT,), pb, po)
                            cs_, cso = cst_rot.next()
                            ac, aco = acc_rot.next()
                            sg, sgo_ = sig_rot.next()
                            copy("pool", cs_[:, 0:3], mlhalo[:, blk, :], (o_mlhalo,), (cso,))
                            copy("act", cs_[:, 3:3 + T], pb[:, 0:T], (po,), (cso,))
                            w_after()

                            def cw(k, blk=blk):
                                i = l * 64 + k * 16 + blk
                                return cml[:, i:i + 1]

                            ts("dve", ac, cs_[:, 3:3 + T], cw(3), ALU.mult, (cso,), (aco,))
                            for k in range(3):
                                stt(ac, cs_[:, k:k + T], cw(k), ac, ALU.mult, ALU.add, (cso, aco), (aco,))
                            copy("pool", mlhalo[:, blk, :], cs_[:, T:T + 3], (cso,), (o_mlhalo,))
                            act(sg, ac, AF.Sigmoid, (aco,), (sgo_,))
                            stt(qm[:, blk, :], ac, (1.0 / 16.0) if blk < 8 else 1.0, sg, ALU.mult, ALU.mult,
                                (aco, sgo_), (o_qm,))
                        for cch in range(8):
                            pb, po = psb[cch % 2], pso[cch % 2]
                            proj(("in", 48 + cch), 16, h_rhs, (o_hT,), pb, po)
                            vs, vso = vst2_rot.next()
                            copy("dve", vs, pb[:, 0:T], (po,), (vso,))
                            w_after()
                            for sub in range(NSUB):
                                tr(psbf[:, sub * 128:(sub + 1) * 128], vs[:, sub * 128:(sub + 1) * 128], cb(IDF),
                                   (vso, o_cst), (pso[7],))
                            hd = cch // 2
                            off = (cch % 2) * 128
                            copy("act", vext[:, :, hd, off:off + 128],
                                 psbf[:, 0:512].rearrange("p (s c) -> p s c", s=NSUB), (pso[7],), (o_vext,))
                        for cch in range(8):
                            pb, po = psb[cch % 2], pso[cch % 2]
                            proj(("in", 56 + cch), 16, h_rhs, (o_hT,), pb, po)
                            act(sgo[:, cch, :], pb[:, 0:T], AF.Sigmoid, (po,), (o_sgo,))
                            w_after()
                        S_.fence(4)
                        pc_.close()
                        bc_s = sb("bc_s", [128, 2, 3, T], F32, pl_)
                        bc_objs = [Obj("bc0"), Obj("bc1")]
                        sm_s = sb("sm_s", [128, 2, 2, 6, 128], F32, pl_)
                        sm_rot = [Rot([sm_s[:, g, i] for i in range(2)], f"sm_s{g}") for g in range(2)]
                        qkw_s = sb("qkw_s", [128, 2, 2, 128], BF16, pl_)
                        qkw_rot = [Rot([qkw_s[:, g, i] for i in range(2)], f"qkw_s{g}") for g in range(2)]
                        qt_s = sb("qt_s", [128, 2, 2, 2, 128], BF16, pl_)
                        qt_rot = [Rot([qt_s[:, g, i] for i in range(2)], f"qt_s{g}") for g in range(2)]
                        kw_s = sb("kw_s", [128, 2, 2, 256], BF16, pl_)
                        kw_rot = [Rot([kw_s[:, g, i] for i in range(2)], f"kw_s{g}") for g in range(2)]
                        ws_s = sb("ws_s", [128, 2, 2], F32, pl_)
                        ws_rot = [Rot([ws_s[:, g, i:i + 1] for i in range(2)], f"ws_s{g}") for g in range(2)]
                        wsl, wo = w_take(("in", 112))
                        for kc in range(16):
                            mm(psb[0][0:4, 0:T], wsl[:, kc, 0:4], hT[:, kc, :], kc == 0, kc == 15, (wo, o_hT), (pso[0],))
                        for kc in range(16):
                            mm(psb[1][0:4, 0:T], wsl[:, kc, 4:8], hT[:, kc, :], kc == 0, kc == 15, (wo, o_hT), (pso[1],))
                        w_after()
                        R_I, R_F, R_B, R_G, R_M = range(5)
                        R_MS, R_WI, R_EM = R_I, R_F, R_B
                        rr = (o_rws,)
                        act(rws[:, R_I, :], psb[0][0:4, 0:T], AF.Identity, (pso[0], o_cst), rr, bias=igb[:, l:l + 1])
                        act(rws[:, R_F, :], psb[1][0:4, 0:T], AF.Identity, (pso[1], o_cst), rr, bias=fgb[:, l:l + 1])
                        act(rws[:, R_F, :], rws[:, R_F, :], AF.Exp, rr, rr, scale=-1.0)
                        act(rws[:, R_F, :], rws[:, R_F, :], AF.Ln, rr, rr, bias=1.0)
                        op("dve", lambda h: h.tensor_tensor_scan(out=rws[:, R_B, :], data0=ones_row[:], data1=rws[:, R_F, :],
                                                                 initial=bcar[:, 0:1], op0=ALU.mult, op1=ALU.add),
                           (o_rws, o_car, o_cst), rr)
                        tt("dve", rws[:, R_G, :], rws[:, R_I, :], rws[:, R_B, :], ALU.add, rr, rr)
                        copy("dve", bcar[:, 0:1], rws[:, R_B, T - 1:T], rr, (o_car,))
                        op("dve", lambda h: h.tensor_tensor_scan(out=rws[:, R_M, :], data0=ones_row[:], data1=rws[:, R_G, :],
                                                                 initial=mcar[:, 0:1], op0=ALU.mult, op1=ALU.max),
                           (o_rws, o_car, o_cst), rr)
                        for c in range(NSUB):
                            scal = mcar[:, 0:1] if c == 0 else rws[:, R_M, c * 128 - 1:c * 128]
                            ts("dve", rws[:, R_MS, c * 128:(c + 1) * 128], zero_row[:, 0:128], scal, ALU.add,
                               (o_rws, o_car, o_cst), rr)
                        copy("dve", mcar[:, 0:1], rws[:, R_M, T - 1:T], rr, (o_car,))
                        tt("dve", rws[:, R_WI, :], rws[:, R_MS, :], rws[:, R_M, :], ALU.subtract, rr, rr)
                        act(rws[:, R_WI, :], rws[:, R_WI, :], AF.Exp, rr, rr)
                        tt("dve", rws[:, R_EM, :], rws[:, R_B, :], rws[:, R_M, :], ALU.subtract, rr, rr)
                        act(rws[:, R_EM, :], rws[:, R_EM, :], AF.Exp, rr, rr)
                        for c in range(NSUB):
                            tr(psb[2][:, c * 4:(c + 1) * 4], rws[:, R_G, c * 128:(c + 1) * 128], cf(IDF)[0:4, 0:4],
                               (o_rws, o_cst), (pso[2],))
                        copy("dve", gcol[:], psb[2][:, 0:16], (pso[2],), (o_gcol,))
                        o_qk = [Obj("qk0"), Obj("qk1")]
                        o_kt = [Obj("kt0"), Obj("kt1")]

                        def ml_gen(hd, g):
                            bc, bco = bc_s[:, g], bc_objs[g]
                            for qi, (rw, scl) in enumerate(((R_M, -1.0), (R_WI, 1.0), (R_EM, 1.0))):
                                pb, po = (psb[2], pso[2]) if qi % 2 == 0 else (psb[6], pso[6])
                                mm(pb[:, 0:T], sel_f[:, hd * 128:(hd + 1) * 128], rws[:, rw, :], True, True,
                                   (o_rws, o_cst), (po,))
                                act(bc[:, qi, :], pb[:, 0:T], AF.Copy, (po,), (bco,), scale=scl)
                            negM = bc[:, 0, :]
                            wib = bc[:, 1, :]
                            emb = bc[:, 2, :]
                            yield
                            for c in range(NSUB):
                                cs = slice(c * 128, (c + 1) * 128)
                                gc = gcol[:, c * 4 + hd:c * 4 + hd + 1]
                                sm, smo = sm_rot[g].next()
                                wi_t, wim_t, dmx, rdn, hh0, hh1 = (sm[:, i, :] for i in range(6))
                                pQK, oQK = psb[5][:, g * 128:(g + 1) * 128], o_qk[g]
                                for kc in range(2):
                                    mm(pQK, qm[:, 8 + 2 * hd + kc, cs], qm[:, 2 * hd + kc, cs], kc == 0, kc == 1,
                                       (o_qm, pso[5]), (oQK,))
                                act(wi_t, negM[:, cs], AF.Exp, (bco, o_gcol), (smo,), bias=gc)
                                tt("pool", wim_t, wi_t, cf(MLE), ALU.mult, (smo, o_cst), (smo,))
                                qkw, qkwo = qkw_rot[g].next()
                                tt("dve", qkw, pQK, wim_t, ALU.mult, (oQK, smo, pso[5]), (qkwo,))
                                qt, qto = qt_rot[g].next()
                                for kc in range(2):
                                    tt("pool", qt[:, kc, :], qm[:, 2 * hd + kc, cs], wib[:, cs], ALU.mult, (o_qm, bco),
                                       (qto,))
                                ws, wso = ws_rot[g].next()
                                act(ws, gc, AF.Exp, (o_gcol, bco), (wso,), bias=negM[:, c * 128 + 127:c * 128 + 128])
                                pKT = psbf[:, g * 256:(g + 1) * 256]
                                for kc in range(2):
                                    tr(pKT[:, kc * 128:(kc + 1) * 128], qm[:, 8 + 2 * hd + kc, cs], cb(IDF),
                                       (o_qm, o_cst, pso[7]), (o_kt[g],))
                                kw, kwo = kw_rot[g].next()
                                act(kw, pKT, AF.Copy, (o_kt[g], wso, pso[7]), (kwo,), scale=ws)
                                yield
                                pN, oN = psb[3 + g], pso[3 + g]
                                use_state = not (first_tile and c == 0)
                                for vch in range(3):
                                    vsl = slice(vch * 128, (vch + 1) * 128)
                                    if use_state:
                                        for kc in range(2):
                                            mm(pN[:, vsl], cbf[:, hd, kc, vsl], qt[:, kc, :], kc == 0, False,
                                               (o_cbf[hd], qto), (oN,))
                                    mm(pN[:, vsl], vext[:, c, hd, vsl], qkw, not use_state, True, (o_vext, qkwo), (oN,))
                                yield
                                act(dmx, pN[:, 256:384], AF.Abs, (oN,), (smo,))
                                tt("dve", dmx, dmx, emb[:, cs], ALU.max, (smo, bco), (smo,))
                                op("dve", lambda h: h.reciprocal(out=rdn, in_=dmx), (smo,), (smo,))
                                for vch, hb in ((0, hh0), (1, hh1)):
                                    vsl = slice(vch * 128, (vch + 1) * 128)
                                    tt("dve", hb, pN[:, vsl], rdn, ALU.mult, (oN, smo), (smo,))
                                    tt("pool", ybr[:, 2, 2 * hd + vch, cs], hb, sgo[:, 2 * hd + vch, cs], ALU.mult,
                                       (smo, o_sgo), (o_ybr[2],))
                                wprev = wib[:, c * 128 + 127:c * 128 + 128]
                                for kc in range(2):
                                    pC, oC = psb[kc], pso[kc]
                                    mm(pC[:, 0:384], kw[:, kc * 128:(kc + 1) * 128], vext[:, c, hd, :], True, True,
                                       (kwo, o_vext), (oC,))
                                    stt(cstate[:, hd, kc, :], cstate[:, hd, kc, :], wprev, pC[:, 0:384], ALU.mult, ALU.add,
                                        (o_cstate[hd], bco, oC), (o_cstate[hd],))
                                copy("act", cbf[:, hd], cstate[:, hd], (o_cstate[hd],), (o_cbf[hd],))
                                yield

                        for h0 in (0, 2):
                            run_rr([ml_gen(h0 + g, g) for g in range(2)])
                        S_.fence(5)

                    if dbg and ti == NT - 1:
                        with ExitStack() as pd_:
                            dbgt = sb("dbgt2", [128, 3, 8, T], F32, pd_)
                            o_dbgt = Obj("dbgt2")
                            copy("dve", dbgt[:], ybr[:], tuple(o_ybr), (o_dbgt,))
                            dma("pool", dbg_out["d_ypool"][:, :, :], dbgt[:, 0], (o_dbgt,), ())
                            dma("pool", dbg_out["d_ysb"][:, :, :], dbgt[:, 1], (o_dbgt,), ())
                            dma("pool", dbg_out["d_yml"][:, :, :], dbgt[:, 2], (o_dbgt,), ())
                            S_.fence(6)

                    merged = sb("merged", [128, 16, T], BF16, pm_)
                    o_merged = Obj("merged")
                    with ExitStack() as pf_:
                        sg_s = sb("sg_s", [128, 2, T], F32, pf_)
                        sg_rot = Rot([sg_s[:, i] for i in range(2)], "sg_s")
                        pr_s = sb("pr_s", [128, 3, T], F32, pf_)
                        pr_rot = Rot([pr_s[:, i] for i in range(3)], "pr_s")
                        mac_s = sb("mac_s", [128, 2, T], F32, pf_)
                        mac_rot = Rot([mac_s[:, i] for i in range(2)], "mac_s")
                        bk = 0
                        for d in range(16):
                            prods = []
                            for br in range(3):
                                pG, oG = psb[bk % 4], pso[bk % 4]
                                pP, oP = psb[4 + bk % 4], pso[4 + bk % 4]
                                bk += 1
                                proj(("gate", br, d), 16, h_rhs, (o_hT,), pG, oG)
                                w_after()
                                proj(("br", br, d), 8, lambda kc, br=br: ybr[:, br, kc, :], (o_ybr[br],), pP, oP)
                                w_after()
                                sg, sgo_ = sg_rot.next()
                                act(sg, pG[:, 0:T], AF.Sigmoid, (oG,), (sgo_,))
                                pr, pro = pr_rot.next()
                                tt("dve", pr, pP[:, 0:T], sg, ALU.mult, (oP, sgo_), (pro,))
                                prods.append((pr, pro))
                            mac, maco = mac_rot.next()
                            tt("pool", mac, prods[0][0], prods[1][0], ALU.add, (prods[0][1], prods[1][1]), (maco,))
                            tt("pool", merged[:, d, :], mac, prods[2][0], ALU.add, (maco, prods[2][1]), (o_merged,))
                        S_.fence(7)
                    if dbg and ti == NT - 1:
                        with ExitStack() as pd_:
                            dbgt = sb("dbgt3", [128, 16, T], F32, pd_)
                            o_dbgt = Obj("dbgt3")
                            copy("dve", dbgt[:], merged[:], (o_merged,), (o_dbgt,))
                            dma("pool", dbg_out["d_merged"][:, :, :], dbgt[:], (o_dbgt,), ())
                            S_.fence(8)

                    def residual_ln(u, o_u, lidx, gseg, final_fn):
                        with ExitStack() as pn_:
                            ub_s = sb("ub_s", [128, 2, T], BF16, pn_)
                            ub_rot = Rot([ub_s[:, i] for i in range(2)], "ub_s")
                            sq_s = sb("sq_s", [128, 2, T], BF16, pn_)
                            sq_rot = Rot([sq_s[:, i] for i in range(2)], "sq_s")
                            st_s = sb("st_s", [128, 4, T], F32, pn_)
                            o_st = Obj("st_s")
                            pS1, oS1 = psb[4], pso[4]
                            pS2, oS2 = psb[5], pso[5]
                            for d in range(16):
                                ub, ubo = ub_rot.next()
                                sq, sqo = sq_rot.next()
                                copy("dve", ub, u[:, d, :], (o_u[d],), (ubo,))
                                act(sq, u[:, d, :], AF.Square, (o_u[d],), (sqo,))
                                mm(pS1[:, 0:T], cb(OND), ub, d == 0, d == 15, (ubo, o_cst), (oS1,))
                                mm(pS2[:, 0:T], cb(OND), sq, d == 0, d == 15, (sqo, o_cst), (oS2,))
                            mean = st_s[:, 0, :]
                            msq = st_s[:, 1, :]
                            var = st_s[:, 2, :]
                            rstd = st_s[:, 3, :]
                            act(mean, pS1[:, 0:T], AF.Copy, (oS1,), (o_st,))
                            act(msq, pS1[:, 0:T], AF.Square, (oS1,), (o_st,))
                            tt("dve", var, pS2[:, 0:T], msq, ALU.subtract, (oS2, o_st), (o_st,))
                            ts("dve", var, var, 0.0, ALU.max, (o_st,), (o_st,))
                            act(var, var, AF.Ln, (o_st,), (o_st,), bias=LN_EPS)
                            act(rstd, var, AF.Exp, (o_st,), (o_st,), scale=-0.5)
                            stt(msq, mean, -1.0, rstd, ALU.mult, ALU.mult, (o_st,), (o_st,))
                            for d in range(16):
                                tt("dve", u[:, d, :], u[:, d, :], rstd, ALU.mult, (o_u[d], o_st), (o_u[d],))
                                tt("dve", u[:, d, :], u[:, d, :], msq, ALU.add, (o_u[d], o_st), (o_u[d],))
                                final_fn(d)
                            S_.fence(9)

                    with ExitStack() as pg_:
                        u = sb("u", [128, 16, T], F32, pg_)
                        o_u = [Obj(f"u{d}") for d in range(16)]
                        x1st = sb("x1st", [128, 2, T], F32, pg_)
                        x1_rot = Rot([x1st[:, i] for i in range(2)], "x1st")
                        for q4 in range(4):
                            dma("pool", u[:, q4 * 4:(q4 + 1) * 4, :], xs_d[src_j, b, :, q4 * 4:(q4 + 1) * 4, t0:t0 + T],
                                (xs_obj[src_j][b][ti],), tuple(o_u[q4 * 4:(q4 + 1) * 4]))
                        for d in range(16):
                            act(u[:, d, :], u[:, d, :], AF.Copy, (o_u[d],), (o_u[d],), scale=ALPHA)
                        for d in range(16):
                            pb, po = psb[d % 4], pso[d % 4]
                            proj(("out", d), 16, lambda kc: merged[:, kc, :], (o_merged,), pb, po)
                            w_after()
                            stt(u[:, d, :], pb[:, 0:T], mcol(l, b, 2, d), u[:, d, :], ALU.mult, ALU.add,
                                (po, o_mod, o_u[d]), (o_u[d],))

                        def fin1(d):
                            xs1, xs1o = x1_rot.next()
                            act(xs1, u[:, d, :], AF.Identity, (o_u[d], o_cst), (xs1o,),
                                bias=lnb[:, l * 32 + d:l * 32 + d + 1], scale=lng[:, l * 32 + d:l * 32 + d + 1])
                            dma("act", x1_d[:, d, :], xs1, (xs1o,), (x1_obj,))
                            if dbg and ti == NT - 1:
                                dma("pool", dbg_out["d_x1"][:, d, :], xs1, (xs1o,), ())
                            act(hT[:, d, :], u[:, d, :], AF.Identity, (o_u[d], o_a2), (o_hT,),
                                bias=a2t[:, 16 + d:17 + d], scale=a2t[:, d:d + 1])

                        residual_ln(u, o_u, l, 2, fin1)
                S_.fence(10)

                with ExitStack() as pf2_:
                    actT = sb("actT", [128, 44, T], BF16, pf2_)
                    o_actT = Obj("actT")
                    with ExitStack() as ph_:
                        ust = sb("ust", [128, 4, 2 + T], F32, ph_)
                        ust_rot = Rot([ust[:, i] for i in range(4)], "ust")
                        fac = sb("fac", [128, 4, T], F32, ph_)
                        fac_rot = Rot([fac[:, i] for i in range(4)], "fac")
                        fsg = sb("fsg", [128, 2, T], F32, ph_)
                        fsg_rot = Rot([fsg[:, i] for i in range(2)], "fsg")
                        bk = 0
                        for p in range(44):
                            accs = []
                            for which in range(2):
                                blk = p + 44 * which
                                pb, po = psb[bk % 4], pso[bk % 4]
                                bk += 1
                                proj(("up", blk), 16, h_rhs, (o_hT,), pb, po)
                                w_after()
                                us, uso = ust_rot.next()
                                ac, aco = fac_rot.next()
                                copy("pool", us[:, 0:2], ffhalo[:, blk, :], (o_ffhalo,), (uso,))
                                copy("act", us[:, 2:2 + T], pb[:, 0:T], (po,), (uso,))

                                def fw(k, blk=blk):
                                    i = l * 264 + k * 88 + blk
                                    return cff[:, i:i + 1]

                                act(ac, pb[:, 0:T], AF.Copy, (po, o_cst), (aco,), scale=fw(2))
                                stt(ac, us[:, 1:1 + T], fw(1), ac, ALU.mult, ALU.add, (uso, aco), (aco,))
                                stt(ac, us[:, 0:T], fw(0), ac, ALU.mult, ALU.add, (uso, aco), (aco,))
                                copy("pool", ffhalo[:, blk, :], us[:, T:T + 2], (uso,), (o_ffhalo,))
                                accs.append((ac, aco))
                            sg, sgo_ = fsg_rot.next()
                            act(sg, accs[1][0], AF.Silu, (accs[1][1],), (sgo_,))
                            tt("dve", actT[:, p, :], sg, accs[0][0], ALU.mult, (sgo_, accs[0][1]), (o_actT,))
                        S_.fence(11)

                    with ExitStack() as pi_:
                        u = sb("u2", [128, 16, T], F32, pi_)
                        o_u = [Obj(f"u2_{d}") for d in range(16)]
                        x2st = sb("x2st", [128, 2, T], F32, pi_)
                        x2_rot = Rot([x2st[:, i] for i in range(2)], "x2st")
                        ost = sb("ost", [128, 2, 512], F32, pi_)
                        ost_rot = Rot([ost[:, i] for i in range(2)], "ost")
                        for q4 in range(4):
                            dma("pool", u[:, q4 * 4:(q4 + 1) * 4, :], x1_d[:, q4 * 4:(q4 + 1) * 4, :], (x1_obj,),
                                tuple(o_u[q4 * 4:(q4 + 1) * 4]))
                        for d in range(16):
                            act(u[:, d, :], u[:, d, :], AF.Copy, (o_u[d],), (o_u[d],), scale=ALPHA)
                        for d in range(16):
                            pb, po = psb[d % 4], pso[d % 4]
                            for pc in range(4):
                                wsl, wo = w_take(("dn", d, pc))
                                for kc in range(11):
                                    mm(pb[:, 0:T], wsl[:, kc, :], actT[:, pc * 11 + kc, :], pc == 0 and kc == 0,
                                       pc == 3 and kc == 10, (wo, o_actT), (po,))
                                w_after()
                            stt(u[:, d, :], pb[:, 0:T], mcol(l, b, 5, d), u[:, d, :], ALU.mult, ALU.add,
                                (po, o_mod, o_u[d]), (o_u[d],))

                        def fin2(d):
                            gcol_ = lng[:, l * 32 + 16 + d:l * 32 + 17 + d]
                            bcol_ = lnb[:, l * 32 + 16 + d:l * 32 + 17 + d]
                            if not last_layer:
                                xs2, xs2o = x2_rot.next()
                                act(xs2, u[:, d, :], AF.Identity, (o_u[d], o_cst), (xs2o,), bias=bcol_, scale=gcol_)
                                dma("act", xs_d[dst_j, b, :, d, t0:t0 + T], xs2, (xs2o,), (xs_obj[dst_j][b][ti],))
                            else:
                                act(u[:, d, :], u[:, d, :], AF.Identity, (o_u[d], o_cst), (o_u[d],), bias=bcol_,
                                    scale=gcol_)

                        residual_ln(u, o_u, l, 5, fin2)
                        if last_layer:
                            for sub in range(NSUB):
                                for g4 in range(4):
                                    pb, po = psb[g4 % 4], pso[g4 % 4]
                                    for j in range(4):
                                        d = g4 * 4 + j
                                        tr(pb[:, j * 128:(j + 1) * 128], u[:, d, sub * 128:(sub + 1) * 128], cf(IDF),
                                           (o_u[d], o_cst), (po,))
                                    os_, oso = ost_rot.next()
                                    copy("act" if g4 % 2 == 0 else "dve", os_, pb[:, 0:512], (po,), (oso,))
                                    r0 = b * S + t0 + sub * 128
                                    dma("pool", out_d[r0:r0 + 128, g4 * 512:(g4 + 1) * 512], os_, (oso,), ())
                        S_.fence(12)
                S_.fence(13)
    S_.barrier(engines=("sp", "pool"))
    es.close()
    return nc, S_.ninst


def host_consts():
    c = np.zeros((128, 9, 128), np.float32)
    i = np.arange(128)
    c[:, 0, :] = np.eye(128)
    c[:, 1, :] = (i[:, None] >= i[None, :])
    c[:, 2, :] = (i[:, None] < i[None, :])
    c[:, 3, :] = (i[:, None] < i[None, :])
    c[:, 4, :] = (i[:, None] <= i[None, :])
    c[:, 5, :] = 1.0
    c[:, 6, :] = 1.0 / D
    sel = np.zeros((4, 4, 128), np.float32)
    for h in range(4):
        sel[h, h, :] = 1.0
    rc = np.zeros((128, 4, 16), np.float32)
    for g, w in enumerate(POOL_W):
        t = np.arange(16)
        rc[:, g, :] = (w / np.minimum(t + 1, w))[None, :]
    return c, sel.reshape(4, 512), rc.reshape(128, 64)


def make_in_maps(inputs, NSEQ, S, L, ncores):
    f = lambda a: np.ascontiguousarray(np.asarray(a, dtype=np.float32))
    cst, sel, rc = host_consts()
    shared = {
        "w_ada": f(inputs["w_ada"]),
        "b_ada": f(inputs["b_ada"]).reshape(L * 96, 128),
        "w_in": f(inputs["w_in"]),
        "conv_ml": f(inputs["conv_ml"]).reshape(L * 64, 128),
        "pool_w": f(inputs["pool_w"]),
        "pool_scale": f(inputs["pool_scale"]).reshape(L * 8, 128),
        "ig_bias": f(inputs["ig_bias"]).reshape(L, 4, 1),
        "fg_bias": f(inputs["fg_bias"]).reshape(L, 4, 1),
        "w_branch": f(inputs["w_branch"]),
        "w_out": f(inputs["w_out"]),
        "w_up": f(inputs["w_up"]),
        "conv_ff": f(inputs["conv_ff"]).reshape(L * 264, 128),
        "w_down": f(inputs["w_down"]),
        "ln_g": f(inputs["ln_g"]).reshape(L * 32, 128),
        "ln_b": f(inputs["ln_b"]).reshape(L * 32, 128),
        "cst": cst, "sel": sel, "rcnt": rc,
    }
    x = f(inputs["x"])
    c = f(inputs["c"])
    maps = []
    for i in range(ncores):
        m = dict(shared)
        m["x"] = np.ascontiguousarray(x[i * NSEQ:(i + 1) * NSEQ].reshape(NSEQ * S, D))
        m["c"] = np.ascontiguousarray(c[i * NSEQ:(i + 1) * NSEQ].reshape(NSEQ * KC, 128))
        maps.append(m)
    return maps


def run(inputs, ncores, dbg=False):
    x = np.asarray(inputs["x"])
    B, S, _ = x.shape
    L = np.asarray(inputs["w_in"]).shape[0]
    NSEQ = B // ncores
    nc, ninst = build_program(NSEQ, S, L, dbg=dbg)
    maps = make_in_maps(inputs, NSEQ, S, L, ncores)
    res = run_bass_kernel_spmd(nc, maps, core_ids=list(range(ncores)))
    out = np.concatenate([np.asarray(r["out"]).reshape(NSEQ, S, D) for r in res.results], axis=0)
    if dbg:
        return out.astype(np.float32), res.results
    return out.astype(np.float32)


def kernel(**inputs):
    return run(inputs, NCORES)
```

```python
import numpy as np
from contextlib import ExitStack
import concourse.bass as bass
import concourse.mybir as mybir
from concourse.bass_utils import run_bass_kernel_spmd

F32 = mybir.dt.float32
BF16 = mybir.dt.bfloat16
AF = mybir.ActivationFunctionType
ALU = mybir.AluOpType

D = 2048
KC = 16
DB = 1024
DFF = 5632
DIN = 14344
T = 512
NSUB = 4
ALPHA = 4.0 ** 0.25
LN_EPS = 1e-5
POOL_W = (2, 4, 8, 16)
SB_SCALE = 128.0 ** -0.5
NCORES = 8


class Obj:
    __slots__ = ("name", "w", "r", "wd")
    fence = {}

    def __init__(self, name):
        self.name = name
        self.w = None
        self.wd = {}
        self.r = dict(Obj.fence)


class EngState:
    def __init__(self, name, h):
        self.name = name
        self.h = h
        self.sem = None
        self.count = 0
        self.seen = {}
        self.own = set()


class Sched:
    SEM_LIMIT = 30000
    NDS = 24

    def __init__(self, nc, es):
        self.nc = nc
        self.es = es
        self.nsem = 0
        self.eng = {}
        for name, h in (("pe", nc.tensor), ("act", nc.scalar), ("dve", nc.vector),
                        ("pool", nc.gpsimd), ("sp", nc.sync)):
            e = EngState(name, h)
            e.sem = self.new_sem("e" + name)
            e.own.add(id(e.sem))
            self.eng[name] = e
        self.dsem = {}
        self.dcnt = {}
        self.dval = {}
        for q in ("sp", "pool", "act"):
            self.dsem[q] = [self.new_sem("d" + q) for _ in range(self.NDS)]
            self.dval[q] = [0] * self.NDS
            self.dcnt[q] = 0
        self.ninst = 0

    def new_sem(self, tag):
        self.nsem += 1
        return self.es.enter_context(self.nc.semaphore(f"{tag}_{self.nsem}"))

    def _collect(self, e, reads, writes, is_dma):
        need = {}

        def add(ev, kind):
            if ev is None:
                return
            s, v = ev
            k = id(s)
            if k in e.own and not is_dma:
                if e.name == "pe":
                    return
            if e.seen.get(k, 0) >= v:
                return
            if k not in need or need[k][1] < v:
                need[k] = (s, v)

        for o in reads:
            add(o.w, "raw")
            for ev in o.wd.values():
                add(ev, "raw")
        for o in writes:
            add(o.w, "waw")
            for ev in o.wd.values():
                add(ev, "waw")
            for ev in o.r.values():
                add(ev, "war")
        return need

    def _wait(self, e, need):
        for k, (s, v) in need.items():
            e.h.wait_ge(s, v)
            e.seen[k] = v

    def _record(self, ev, reads, writes, is_dma=False):
        k = id(ev[0])
        for o in writes:
            if is_dma:
                o.wd[k] = ev
            else:
                o.w = ev
                o.wd = {}
            o.r = {}
        for o in reads:
            o.r[k] = ev

    def fence(self, site=-1):
        self.barrier()
        Obj.fence = {id(s): (s, v) for s, v in self.all_events()}

    def op(self, engname, fn, reads=(), writes=()):
        e = self.eng[engname]
        self._wait(e, self._collect(e, reads, writes, False))
        inst = fn(e.h)
        e.count += 1
        inst.then_inc(e.sem, 1)
        ev = (e.sem, e.count)
        self._record(ev, reads, writes)
        self.ninst += 1
        if e.count >= self.SEM_LIMIT:
            e.sem = self.new_sem("e" + engname)
            e.own.add(id(e.sem))
            e.count = 0
        return ev

    def dma(self, q, out, in_, reads=(), writes=(), **kw):
        e = self.eng[q]
        i = self.dcnt[q] % self.NDS
        self.dcnt[q] += 1
        s = self.dsem[q][i]
        need = self._collect(e, reads, writes, True)
        pv = self.dval[q][i]
        if pv > 0 and e.seen.get(id(s), 0) < pv:
            need[id(s)] = (s, pv)
        self._wait(e, need)
        e.h.dma_start(out=out, in_=in_, **kw).then_inc(s, 16)
        self.dval[q][i] = pv + 16
        ev = (s, pv + 16)
        self._record(ev, reads, writes, True)
        self.ninst += 1
        return ev

    def all_events(self):
        evs = []
        for e in self.eng.values():
            if e.count > 0:
                evs.append((e.sem, e.count))
        for q in self.dsem:
            for i, s in enumerate(self.dsem[q]):
                if self.dval[q][i] > 0:
                    evs.append((s, self.dval[q][i]))
        return evs

    def barrier(self, engines=("pe", "act", "dve", "pool", "sp")):
        evs = self.all_events()
        for n in engines:
            e = self.eng[n]
            for s, v in evs:
                k = id(s)
                if e.seen.get(k, 0) >= v:
                    continue
                if k in e.own and (s is e.sem) and n == "pe":
                    pass
                e.h.wait_ge(s, v)
                e.seen[k] = v


def run_rr(gens):
    active = list(gens)
    while active:
        for g in list(active):
            try:
                next(g)
            except StopIteration:
                active.remove(g)


class Rot:
    def __init__(self, aps, name):
        self.aps = aps
        self.objs = [Obj(f"{name}{i}") for i in range(len(aps))]
        self.i = 0

    def next(self):
        i = self.i % len(self.aps)
        self.i += 1
        return self.aps[i], self.objs[i]


def build_program(NSEQ, S, L, dbg=False):
    Obj.fence = {}
    nc = bass.Bass("TRN2", target_bir_lowering=False)
    es = ExitStack()
    NT = S // T
    NKT = S // 128

    def din(name, shape):
        return nc.dram_tensor(name, list(shape), F32, kind="ExternalInput").ap()

    x_d = din("x", [NSEQ * S, D])
    c_d = din("c", [NSEQ * KC, 128])
    w_ada = din("w_ada", [L, D, 6 * D])
    b_ada = din("b_ada", [L * 96, 128])
    w_in = din("w_in", [L, D, DIN])
    conv_ml = din("conv_ml", [L * 4 * 16, 128])
    pool_w = din("pool_w", [L, 4, 256, 256])
    pool_scale = din("pool_scale", [L * 8, 128])
    ig_bias = din("ig_bias", [L, 4, 1])
    fg_bias = din("fg_bias", [L, 4, 1])
    w_branch = din("w_branch", [L, 3, DB, D])
    w_out = din("w_out", [L, D, D])
    w_up = din("w_up", [L, D, 2 * DFF])
    conv_ff = din("conv_ff", [L * 3 * 88, 128])
    w_down = din("w_down", [L, DFF, D])
    ln_g = din("ln_g", [L * 2 * 16, 128])
    ln_b = din("ln_b", [L * 2 * 16, 128])
    cst_d = din("cst", [128, 9, 128])
    sel_d = din("sel", [4, 4 * 128])
    rcnt_d = din("rcnt", [128, 4 * 16])
    out_d = nc.dram_tensor("out", [NSEQ * S, D], F32, kind="ExternalOutput").ap()

    def dscr(name, shape, dt):
        return nc.dram_tensor(name, list(shape), dt, kind="Internal").ap()

    wb_in = dscr("wb_in", [L, 113, 128, 16, 128], BF16)
    wb_br = dscr("wb_br", [L, 3, 16, 128, 8, 128], BF16)
    wb_out = dscr("wb_out", [L, 16, 128, 16, 128], BF16)
    wb_up = dscr("wb_up", [L, 88, 128, 16, 128], BF16)
    wb_dn = dscr("wb_dn", [L, 16, 4, 128, 11, 128], BF16)
    xs_d = dscr("xs", [2, NSEQ, 128, 16, S], F32)
    x1_d = dscr("x1s", [128, 16, T], F32)
    k_d = dscr("kscr", [8, 128, S], BF16)
    v_d = dscr("vscr", [NKT, 128, DB], BF16)
    xs_obj = [[[Obj(f"xs{j}_{b}_{t}") for t in range(NT)] for b in range(NSEQ)] for j in range(2)]
    x1_obj = Obj("x1s")
    k_obj = Obj("kscr")
    v_obj = Obj("vscr")
    wb_obj = Obj("wb")

    dbg_out = {}
    if dbg:
        for nm, shp in (("d_h", [128, 16, T]), ("d_ypool", [128, 8, T]), ("d_ysb", [128, 8, T]),
                        ("d_yml", [128, 8, T]), ("d_merged", [128, 16, T]), ("d_x1", [128, 16, T]),
                        ("d_mod", [128, L * NSEQ * 96])):
            dbg_out[nm] = nc.dram_tensor(nm, shp, F32, kind="ExternalOutput").ap()

    Obj.fence = {}
    S_ = Sched(nc, es)
    op = S_.op
    dma = S_.dma

    sbn = [0]

    def sb(name, shape, dt, stack=es):
        sbn[0] += 1
        return stack.enter_context(nc.sbuf_tensor(f"{name}_{sbn[0]}", list(shape), dt))

    psb = [es.enter_context(nc.psum_tensor(f"ps{i}", [128, 512], F32)) for i in range(8)]
    pso = [Obj(f"ps{i}") for i in range(8)]

    cst_f = sb("cst_f", [128, 9, 128], F32)
    cst_b = sb("cst_b", [128, 9, 128], BF16)
    IDF, UIN, LST, SMK, MLE, ONE, OND = 0, 1, 2, 3, 4, 5, 6
    sel_f = sb("sel_f", [4, 4 * 128], F32)
    rcnt_f = sb("rcnt_f", [128, 4 * 16], F32)
    ones_row = sb("ones_row", [4, T], F32)
    zero_row = sb("zero_row", [4, T], F32)
    modt = sb("modt", [128, L * NSEQ * 96], F32)
    a2t = sb("a2t", [128, 2 * 16], F32)
    cml = sb("cml", [128, L * 64], F32)
    cff = sb("cff", [128, L * 264], F32)
    psc = sb("psc", [128, L * 8], F32)
    lng = sb("lng", [128, L * 32], F32)
    lnb = sb("lnb", [128, L * 32], F32)
    igb = sb("igb", [4, L], F32)
    fgb = sb("fgb", [4, L], F32)
    poolw_b = sb("poolw_b", [128, 4, 2, 256], BF16)
    hT = sb("hT", [128, 16, T], BF16)
    NSLOT = 6
    wring = sb("wring", [128, NSLOT, 16, 128], BF16)
    cstate = sb("cstate", [128, 4, 2, 384], F32)
    cbf = sb("cbf", [128, 4, 2, 384], BF16)
    pohalo = sb("pohalo", [128, 8, 15], F32)
    mlhalo = sb("mlhalo", [128, 16, 3], F32)
    ffhalo = sb("ffhalo", [128, 88, 2], F32)
    bcar = sb("bcar", [4, 1], F32)
    mcar = sb("mcar", [4, 1], F32)

    o_cst = Obj("cst")
    o_mod = Obj("mod")
    o_hT = Obj("hT")
    o_cstate = [Obj(f"cst{h}") for h in range(4)]
    o_cbf = [Obj(f"cbf{h}") for h in range(4)]
    o_pohalo = Obj("pohalo")
    o_mlhalo = Obj("mlhalo")
    o_ffhalo = Obj("ffhalo")
    o_car = Obj("car")
    o_poolw = Obj("poolw")
    o_a2 = Obj("a2")
    wslot_obj = [Obj(f"wslot{i}") for i in range(NSLOT)]

    def cb(i):
        return cst_b[:, i, :]

    def cf(i):
        return cst_f[:, i, :]

    def mcol(l, b, seg, ch):
        i = ((l * NSEQ + b) * 6 + seg) * 16 + ch
        return modt[:, i:i + 1]

    def act(out, in_, func, reads, writes, bias=None, scale=None):
        kw = {}
        if bias is not None:
            kw["bias"] = bias
        if scale is not None:
            kw["scale"] = scale
        return op("act", lambda h: h.activation(out=out, in_=in_, func=func, **kw), reads, writes)

    def tt(eng, out, in0, in1, alu, reads, writes):
        return op(eng, lambda h: h.tensor_tensor(out=out, in0=in0, in1=in1, op=alu), reads, writes)

    def ts(eng, out, in0, s1, op0, reads, writes, s2=None, op1=None):
        if op1 is None:
            return op(eng, lambda h: h.tensor_scalar(out=out, in0=in0, scalar1=s1, scalar2=None, op0=op0),
                      reads, writes)
        return op(eng, lambda h: h.tensor_scalar(out=out, in0=in0, scalar1=s1, scalar2=s2, op0=op0, op1=op1),
                  reads, writes)

    def stt(out, in0, scalar, in1, op0, op1, reads, writes):
        return op("dve", lambda h: h.scalar_tensor_tensor(out=out, in0=in0, scalar=scalar, in1=in1,
                                                          op0=op0, op1=op1), reads, writes)

    def copy(eng, out, in_, reads, writes):
        if eng == "act":
            return act(out, in_, AF.Copy, reads, writes)
        return op(eng, lambda h: h.tensor_copy(out=out, in_=in_), reads, writes)

    def memset(eng, ap, val, writes):
        return op(eng, lambda h: h.memset(ap, val), (), writes)

    def mm(out, lhsT, rhs, start, stop, reads, writes, skip=False):
        if skip:
            return op("pe", lambda h: h.matmul(out, lhsT, rhs, start=start, stop=stop, skip_group_check=True),
                      reads, writes)
        return op("pe", lambda h: h.matmul(out, lhsT, rhs, start=start, stop=stop), reads, writes)

    def tr(out, in_, ident, reads, writes):
        return op("pe", lambda h: h.transpose(out, in_, ident), reads, writes)

    dma("sp", cst_f[:], cst_d[:, :, :], (), (o_cst,))
    dma("sp", sel_f[:], sel_d[:, :], (), (o_cst,))
    dma("sp", rcnt_f[:], rcnt_d[:, :], (), (o_cst,))
    for l in range(L):
        dma("sp", igb[:, l:l + 1], ig_bias[l], (), (o_cst,))
        dma("sp", fgb[:, l:l + 1], fg_bias[l], (), (o_cst,))
    copy("dve", cst_b[:], cst_f[:], (o_cst,), (o_cst,))
    memset("dve", ones_row[:], 1.0, (o_cst,))
    memset("dve", zero_row[:], 0.0, (o_cst,))

    with ExitStack() as ps_:
        rows = sb("rows", [128, 128], F32, ps_)
        o_rows = Obj("rows")
        cT = sb("cT", [128, 16, NSEQ], F32, ps_)
        bada = sb("bada", [128, L * 96], F32, ps_)
        o_small = Obj("small")

        def vec_cols(src_ap, nrows, dst_ap, post=None):
            dma("sp", rows[0:nrows, :], src_ap, (), (o_rows,))
            tr(psb[0][:, 0:nrows], rows[0:nrows, :], cf(IDF)[0:nrows, 0:nrows], (o_rows, o_cst), (pso[0],))
            if post is None:
                copy("dve", dst_ap, psb[0][:, 0:nrows], (pso[0],), (o_small,))
            else:
                post(dst_ap, psb[0][:, 0:nrows])

        def chunks(n, m=128):
            i = 0
            while i < n:
                yield i, min(m, n - i)
                i += m

        for i, n in chunks(L * 96):
            vec_cols(b_ada[i:i + n, :], n, bada[:, i:i + n])
        for i, n in chunks(L * 64):
            vec_cols(conv_ml[i:i + n, :], n, cml[:, i:i + n])
        for i, n in chunks(L * 264):
            vec_cols(conv_ff[i:i + n, :], n, cff[:, i:i + n])
        vec_cols(pool_scale[:, :], L * 8, psc[:, :])
        vec_cols(ln_g[:, :], L * 32, lng[:, :])
        vec_cols(ln_b[:, :], L * 32, lnb[:, :])

        def post_c(dst, src):
            sg = sb("csig", [128, NSEQ * 16], F32, ps_)
            act(sg[:], src, AF.Sigmoid, (pso[0],), (o_small,))
            tt("dve", dst, sg[:].rearrange("p (b k) -> p k b", b=NSEQ), src.rearrange("p (b k) -> p k b", b=NSEQ),
               ALU.mult, (pso[0], o_small), (o_small,))

        vec_cols(c_d[:, :], NSEQ * 16, cT[:], post=post_c)

        wada_s = sb("wada_s", [128, 3, 16, 128], F32, ps_)
        wada_rot = Rot([wada_s[:, i] for i in range(3)], "wada")
        def mod_gen():
            bank = 0
            for l in range(L):
                for j in range(96):
                    slot, so = wada_rot.next()
                    dma("sp", slot, w_ada[l, :, j * 128:(j + 1) * 128].rearrange("(k p) c -> p k c", p=128), (), (so,))
                    pb, po = psb[bank % 4], pso[bank % 4]
                    bank += 1
                    for kc in range(16):
                        mm(pb[:, 0:NSEQ], slot[:, kc, :], cT[:, kc, :], kc == 0, kc == 15, (so, o_small), (po,))
                    seg = j // 16
                    ch = j % 16
                    for b in range(NSEQ):
                        addc = 1.0 if seg in (1, 2, 4, 5) else 0.0
                        ts("dve", mcol(l, b, seg, ch), pb[:, b:b + 1], bada[:, l * 96 + j:l * 96 + j + 1], ALU.add,
                           (po, o_small), (o_mod,), s2=addc, op1=ALU.add)
                    if j % 3 == 2:
                        yield
            if dbg:
                dma("sp", dbg_out["d_mod"][:, :], modt[:], (o_mod,), ())

        xin_t = sb("xin_t", [128, 2, D], F32, ps_)
        xin_rot = Rot([xin_t[:, i] for i in range(2)], "xin_t")
        xo_t = sb("xo_t", [128, 2, 16, 128], F32, ps_)
        xo_rot = Rot([xo_t[:, i] for i in range(2)], "xo_t")

        def xt_gen():
            for b in range(NSEQ):
                for r in range(NKT):
                    xi, xio = xin_rot.next()
                    xo, xoo = xo_rot.next()
                    dma("pool", xi, x_d[b * S + r * 128:b * S + (r + 1) * 128, :], (), (xio,))
                    for g4 in range(4):
                        pb, po = psb[4 + g4], pso[4 + g4]
                        for j in range(4):
                            ch = g4 * 4 + j
                            tr(pb[:, j * 128:(j + 1) * 128], xi[:, ch * 128:(ch + 1) * 128], cf(IDF), (xio, o_cst), (po,))
                        copy("act", xo[:, g4 * 4:(g4 + 1) * 4, :],
                             pb[:, :].rearrange("p (a b) -> p a b", a=4), (po,), (xoo,))
                    dma("pool", xs_d[0, b, :, :, r * 128:(r + 1) * 128], xo, (xoo,), (xs_obj[0][b][(r * 128) // T],))
                    yield

        run_rr([mod_gen(), xt_gen()])
        S_.barrier()

    def f32src(l, key):
        if key[0] == "in":
            bb = key[1]
            if bb == 112:
                return w_in[l, :, 8192:8200], 8
            return w_in[l, :, bb * 128:(bb + 1) * 128], 128
        if key[0] == "gate":
            c0 = 8200 + (key[1] * 16 + key[2]) * 128
            return w_in[l, :, c0:c0 + 128], 128
        if key[0] == "br":
            return w_branch[l, key[1], :, key[2] * 128:(key[2] + 1) * 128], 128
        if key[0] == "out":
            return w_out[l, :, key[1] * 128:(key[1] + 1) * 128], 128
        if key[0] == "up":
            return w_up[l, :, key[1] * 128:(key[1] + 1) * 128], 128
        if key[0] == "dn":
            return w_down[l, key[2] * 1408:(key[2] + 1) * 1408, key[1] * 128:(key[1] + 1) * 128], 128
        raise KeyError(key)

    def tile_plan(l, jit):
        plan = []
        for b in range(64):
            plan.append((("in", b), wb_in[l, b], 16))
        plan.append((("in", 112), wb_in[l, 112], 16))
        for d in range(16):
            for br in range(3):
                plan.append((("gate", br, d), wb_in[l, 64 + br * 16 + d], 16))
                plan.append((("br", br, d), wb_br[l, br, d], 8))
        for d in range(16):
            plan.append((("out", d), wb_out[l, d], 16))
        for p in range(44):
            plan.append((("up", p), wb_up[l, p], 16))
            plan.append((("up", 44 + p), wb_up[l, 44 + p], 16))
        for d in range(16):
            for pc in range(4):
                plan.append((("dn", d, pc), wb_dn[l, d, pc], 11))
        out = []
        for key, dst, kcn in plan:
            if jit:
                src, ncols = f32src(l, key)
                out.append((key, dst, kcn, src.rearrange("(k p) c -> p k c", p=128), ncols, l))
            else:
                out.append((key, dst, kcn, None, 128, l))
        return out

    gplan = []
    for b in range(NSEQ):
        for l in range(L):
            for ti in range(NT):
                gplan.extend(tile_plan(l, b == 0 and ti == 0))
    wst = {"issued": 0, "ptr": 0, "cast": 0, "ce": 0}
    NSTG = 2
    jstg = sb("jstg", [128, NSTG, 16, 128], F32)
    jstg_obj = [Obj(f"jstg{i}") for i in range(NSTG)]
    wbl_obj = [Obj(f"wb{l}") for l in range(L)]

    def w_issue_upto(n):
        while wst["issued"] < min(n, len(gplan)):
            i = wst["issued"]
            key, dst, kcn, src32, ncols, l_ = gplan[i]
            sl = i % NSLOT
            if src32 is None:
                dma("sp", wring[:, sl, 0:kcn, :], dst, (wbl_obj[l_],), (wslot_obj[sl],))
            else:
                if i >= wst["cast"] + NSTG:
                    return
                st = i % NSTG
                dma("sp", jstg[:, st, 0:kcn, 0:ncols], src32, (), (jstg_obj[st],))
            wst["issued"] += 1

    def w_cast_upto(n):
        while wst["cast"] < min(n, len(gplan), wst["issued"]):
            i = wst["cast"]
            key, dst, kcn, src32, ncols, l_ = gplan[i]
            if src32 is not None:
                sl = i % NSLOT
                st = i % NSTG
                eng = "act" if wst["ce"] % 2 == 0 else "dve"
                wst["ce"] += 1
                copy(eng, wring[:, sl, 0:kcn, 0:ncols], jstg[:, st, 0:kcn, 0:ncols], (jstg_obj[st],), (wslot_obj[sl],))
                if ncols == 128:
                    dma("pool", dst, wring[:, sl, 0:kcn, :], (wslot_obj[sl],), (wbl_obj[l_],))
                else:
                    dma("pool", dst[:, :, 0:ncols], wring[:, sl, 0:kcn, 0:ncols], (wslot_obj[sl],), (wbl_obj[l_],))
            wst["cast"] += 1

    def w_take(key):
        i = wst["ptr"]
        assert gplan[i][0] == key, (gplan[i][0], key)
        for _ in range(NSLOT + 2):
            w_cast_upto(i + 1)
            w_issue_upto(i + NSLOT - 1)
        w_cast_upto(i + 1)
        assert wst["issued"] > i and wst["cast"] > i
        wst["ptr"] += 1
        sl = i % NSLOT
        return wring[:, sl], wslot_obj[sl]

    def w_after():
        w_cast_upto(wst["ptr"] + NSLOT - 3)
        w_issue_upto(wst["ptr"] + NSLOT - 1)
        w_cast_upto(wst["ptr"] + NSLOT - 3)

    def proj(key, kcn, rhs_fn, rhs_objs, pb, po, m=128):
        wsl, wo = w_take(key)
        for kc in range(kcn):
            mm(pb[0:m, 0:T], wsl[:, kc, 0:m], rhs_fn(kc), kc == 0, kc == kcn - 1, (wo,) + tuple(rhs_objs), (po,))

    for b in range(NSEQ):
        for l in range(L):
            last_layer = (l == L - 1)
            with ExitStack() as st_:
                pwf = sb("pwf", [128, 4, 2, 256], F32, st_)
                o_pwf = Obj("pwf")
                dma("sp", pwf[:], pool_w[l].rearrange("g (k p) c -> p g k c", p=128), (), (o_pwf,))
                copy("dve", poolw_b[:], pwf[:], (o_pwf,), (o_poolw,))
                for ch in range(16):
                    g1c = lng[:, l * 32 + ch:l * 32 + ch + 1]
                    b1c = lnb[:, l * 32 + ch:l * 32 + ch + 1]
                    tt("dve", a2t[:, ch:ch + 1], g1c, mcol(l, b, 4, ch), ALU.mult, (o_mod,), (o_a2,))
                    stt(a2t[:, 16 + ch:17 + ch], b1c, mcol(l, b, 4, ch), mcol(l, b, 3, ch), ALU.mult, ALU.add,
                        (o_mod,), (o_a2,))
                S_.fence(0)
            memset("dve", cstate[:], 0.0, tuple(o_cstate))
            memset("dve", cbf[:], 0.0, tuple(o_cbf))
            memset("dve", pohalo[:], 0.0, (o_pohalo,))
            memset("dve", mlhalo[:], 0.0, (o_mlhalo,))
            memset("dve", ffhalo[:], 0.0, (o_ffhalo,))
            memset("dve", bcar[:], 0.0, (o_car,))
            memset("dve", mcar[:], 0.0, (o_car,))

            for ti in range(NT):
                t0 = ti * T
                first_tile = (ti == 0)
                src_j = l % 2
                dst_j = (l + 1) % 2
                with ExitStack() as pa_:
                    xin = sb("xin", [128, 16, T], F32, pa_)
                    o_xin = Obj("xin")
                    for q4 in range(4):
                        dma("pool", xin[:, q4 * 4:(q4 + 1) * 4, :], xs_d[src_j, b, :, q4 * 4:(q4 + 1) * 4, t0:t0 + T],
                            (xs_obj[src_j][b][ti],), (o_xin,))
                    for ch in range(16):
                        act(hT[:, ch, :], xin[:, ch, :], AF.Identity, (o_xin, o_mod), (o_hT,),
                            bias=mcol(l, b, 0, ch), scale=mcol(l, b, 1, ch))
                    if dbg and ti == NT - 1:
                        dbgt = sb("dbgt", [128, 16, T], F32, pa_)
                        o_dbgt = Obj("dbgt")
                        copy("dve", dbgt[:], hT[:], (o_hT,), (o_dbgt,))
                        dma("pool", dbg_out["d_h"][:, :, :], dbgt[:], (o_dbgt,), ())
                    S_.fence(1)

                def h_rhs(kc):
                    return hT[:, kc, :]

                with ExitStack() as pm_:
                    ybr = sb("ybr", [128, 3, 8, T], BF16, pm_)
                    o_ybr = [Obj(f"ybr{i}") for i in range(3)]
                    with ExitStack() as pp_:
                        apool = sb("apool", [128, 8, 15 + T], F32, pp_)
                        o_apool = [Obj(f"apool{g}") for g in range(4)]
                        ptmp = sb("ptmp", [128, 2, 2, 15 + T], F32, pp_)
                        o_ptmp = [Obj("ptmp0"), Obj("ptmp1")]
                        dTt = sb("dTt", [128, 8, T], BF16, pp_)
                        o_dT = [Obj(f"dT{g}") for g in range(4)]
                        for g in range(4):
                            copy("pool", apool[:, 2 * g:2 * g + 2, 0:15], pohalo[:, 2 * g:2 * g + 2, :], (o_pohalo,),
                                 (o_apool[g],))
                        for cch in range(8):
                            pb, po = psb[cch % 2], pso[cch % 2]
                            proj(("in", cch), 16, h_rhs, (o_hT,), pb, po)
                            copy("act", apool[:, cch, 15:15 + T], pb[:, 0:T], (po,), (o_apool[cch // 2],))
                            w_after()
                        for g in range(4):
                            cur = apool[:, 2 * g:2 * g + 2, :]
                            cur_o = o_apool[g]
                            W = 15 + T
                            for k in range(g + 1):
                                sh = 1 << k
                                dst = ptmp[:, k % 2]
                                dst_o = o_ptmp[k % 2]
                                eng = "pool" if g % 2 == 0 else "dve"
                                lo = 2 * sh - 1
                                tt(eng, dst[:, :, lo:W], cur[:, :, lo:W], cur[:, :, lo - sh:W - sh], ALU.add,
                                   (cur_o,), (dst_o,))
                                cur, cur_o = dst, dst_o
                            if first_tile:
                                rc = rcnt_f[:, g * 16:(g + 1) * 16]
                                for j2 in range(2):
                                    tt("dve", cur[:, j2, 15:31], cur[:, j2, 15:31], rc, ALU.mult, (cur_o, o_cst),
                                       (cur_o,))
                            stt(dTt[:, 2 * g:2 * g + 2, :], cur[:, :, 15:15 + T], 1.0 / POOL_W[g],
                                apool[:, 2 * g:2 * g + 2, 15:15 + T], ALU.mult, ALU.subtract,
                                (cur_o, o_apool[g]), (o_dT[g],))
                            for m in range(2):
                                pb, po = psb[2 + (2 * g + m) % 2], pso[2 + (2 * g + m) % 2]
                                for kc in range(2):
                                    mm(pb[:, 0:T], poolw_b[:, g, kc, m * 128:(m + 1) * 128], dTt[:, 2 * g + kc, :],
                                       kc == 0, kc == 1, (o_poolw, o_dT[g]), (po,))
                                ch = 2 * g + m
                                act(ybr[:, 0, ch, :], pb[:, 0:T], AF.Copy, (po,), (o_ybr[0],),
                                    scale=psc[:, l * 8 + ch:l * 8 + ch + 1])
                        for g in range(4):
                            copy("pool", pohalo[:, 2 * g:2 * g + 2, :], apool[:, 2 * g:2 * g + 2, T:T + 15],
                                 (o_apool[g],), (o_pohalo,))
                        S_.fence(2)

                    with ExitStack() as pa2_:
                        qT = sb("qT", [128, 8, T], BF16, pa2_)
                        o_qT = Obj("qT")
                        kst = sb("kst", [128, 8, T], BF16, pa2_)
                        o_kst = Obj("kst")
                        vst = sb("vst", [128, 2, T], BF16, pa2_)
                        vst_rot = Rot([vst[:, i] for i in range(2)], "vst")
                        vtok = sb("vtok", [128, NSUB, DB], BF16, pa2_)
                        o_vtok = Obj("vtok")
                        kbuf = sb("kbuf", [128, 3, S], BF16, pa2_)
                        kbuf_rot = Rot([kbuf[:, i] for i in range(3)], "kbuf")
                        vbuf = sb("vbuf", [128, 3, NKT, 128], BF16, pa2_)
                        vbuf_rot = Rot([vbuf[:, i] for i in range(3)], "vbuf")
                        G_AT = 2
                        ebuf = sb("ebuf", [128, G_AT, 2, T], F32, pa2_)
                        e_rot = [Rot([ebuf[:, g, i] for i in range(2)], f"ebuf{g}") for g in range(G_AT)]
                        spbuf = sb("spbuf", [128, G_AT, 2, T], BF16, pa2_)
                        sp_rot = [Rot([spbuf[:, g, i] for i in range(2)], f"spbuf{g}") for g in range(G_AT)]
                        tbuf = sb("tbuf", [128, G_AT, 2, T], F32, pa2_)
                        t_rot = [Rot([tbuf[:, g, i] for i in range(2)], f"tbuf{g}") for g in range(G_AT)]
                        abuf = sb("abuf", [128, G_AT, 2, T], BF16, pa2_)
                        a_rot = [Rot([abuf[:, g, i] for i in range(2)], f"abuf{g}") for g in range(G_AT)]
                        for hh in range(8):
                            pb, po = psb[hh % 2], pso[hh % 2]
                            proj(("in", 8 + hh), 16, h_rhs, (o_hT,), pb, po)
                            copy("act", qT[:, hh, :], pb[:, 0:T], (po,), (o_qT,))
                            w_after()
                        for hh in range(8):
                            pb, po = psb[hh % 2], pso[hh % 2]
                            proj(("in", 16 + hh), 16, h_rhs, (o_hT,), pb, po)
                            copy("act", kst[:, hh, :], pb[:, 0:T], (po,), (o_kst,))
                            w_after()
                        dma("pool", k_d[:, :, t0:t0 + T].rearrange("h p t -> p h t"), kst[:], (o_kst,), (k_obj,))
                        psbf = psb[7].bitcast(BF16)
                        for hh in range(8):
                            pb, po = psb[hh % 2], pso[hh % 2]
                            proj(("in", 24 + hh), 16, h_rhs, (o_hT,), pb, po)
                            vs, vso = vst_rot.next()
                            copy("dve", vs, pb[:, 0:T], (po,), (vso,))
                            w_after()
                            for sub in range(NSUB):
                                tr(psbf[:, sub * 128:(sub + 1) * 128], vs[:, sub * 128:(sub + 1) * 128], cb(IDF),
                                   (vso, o_cst), (pso[7],))
                            copy("act", vtok[:, :, hh * 128:(hh + 1) * 128],
                                 psbf[:, 0:512].rearrange("p (s c) -> p s c", s=NSUB), (pso[7],), (o_vtok,))
                        dma("pool", v_d[ti * NSUB:(ti + 1) * NSUB].rearrange("s p c -> p s c"), vtok[:], (o_vtok,),
                            (v_obj,))
                        nk = (ti + 1) * NSUB

                        def attn_gen(hh, g):
                            kb, kbo = kbuf_rot.next()
                            vb, vbo = vbuf_rot.next()
                            dma("pool", kb[:, 0:nk * 128], k_d[hh, :, 0:nk * 128], (k_obj,), (kbo,))
                            dma("pool", vb[:, 0:nk, :], v_d[0:nk, :, hh * 128:(hh + 1) * 128].rearrange("s p c -> p s c"),
                                (v_obj,), (vbo,))
                            pZ, oZ = psb[g], pso[g]
                            pR, oR = psb[2 + g], pso[2 + g]
                            pO, oO = psb[4 + g], pso[4 + g]
                            js = list(range(nk - 1, -1, -1))

                            def qlo_of(j):
                                return max(0, j - ti * NSUB) * 128

                            def zmm(j):
                                ql = qlo_of(j)
                                mm(pZ[:, ql:T], kb[:, j * 128:(j + 1) * 128], qT[:, hh, ql:T], True, True,
                                   (kbo, o_qT), (oZ,))

                            zmm(js[0])
                            yield
                            for idx, j in enumerate(js):
                                dloc = j - ti * NSUB
                                qlo = qlo_of(j)
                                firstj = (idx == 0)
                                ee, eo = e_rot[g].next()
                                sp, spo = sp_rot[g].next()
                                tb, tbo = t_rot[g].next()
                                ab, abo = a_rot[g].next()
                                act(ee[:, qlo:T], pZ[:, qlo:T], AF.Exp, (oZ,), (eo,), scale=SB_SCALE)
                                act(sp[:, qlo:T], ee[:, qlo:T], AF.Ln, (eo,), (spo,), bias=1.0)
                                if dloc >= 0:
                                    tt("pool", sp[:, qlo:qlo + 128], sp[:, qlo:qlo + 128], cb(SMK), ALU.mult,
                                       (spo, o_cst), (spo,))
                                yield
                                mm(pR[:, qlo:T], cb(UIN), sp[:, qlo:T], firstj, True, (spo, o_cst), (oR,), skip=True)
                                if idx + 1 < len(js):
                                    zmm(js[idx + 1])
                                yield
                                act(tb[:, qlo:T], pR[:, qlo:T], AF.Exp, (oR,), (tbo,), scale=-1.0)
                                yield
                                mm(pR[:, qlo:T], cb(LST), sp[:, qlo:T], False, True, (spo, o_cst), (oR,), skip=True)
                                tt("dve", ab[:, qlo:T], ee[:, qlo:T], tb[:, qlo:T], ALU.mult, (eo, tbo), (abo,))
                                if dloc >= 0:
                                    tt("pool", ab[:, qlo:qlo + 128], ab[:, qlo:qlo + 128], cb(SMK), ALU.mult,
                                       (abo, o_cst), (abo,))
                                mm(pO[:, qlo:T], vb[:, j, :], ab[:, qlo:T], firstj, j == 0, (vbo, abo), (oO,), skip=True)
                                yield
                            copy("dve", ybr[:, 1, hh, :], pO[:, 0:T], (oO,), (o_ybr[1],))

                        for h0 in range(0, 8, G_AT):
                            run_rr([attn_gen(h0 + g, g) for g in range(G_AT)])
                        S_.fence(3)

                    with ExitStack() as pl_:
                        qm = sb("qm", [128, 16, T], BF16, pl_)
                        o_qm = Obj("qm")
                        vext = sb("vext", [128, NSUB, 4, 384], BF16, pl_)
                        o_vext = Obj("vext")
                        sgo = sb("sgo", [128, 8, T], BF16, pl_)
                        o_sgo = Obj("sgo")
                        rws = sb("rws", [4, 5, T], F32, pl_)
                        o_rws = Obj("rws")
                        gcol = sb("gcol", [128, 16], F32, pl_)
                        o_gcol = Obj("gcol")
                        pc_ = ExitStack()
                        cst_s = sb("cst_s", [128, 2, 3 + T], F32, pc_)
                        cst_rot = Rot([cst_s[:, i] for i in range(2)], "cst_s")
                        acc_s = sb("acc_s", [128, 2, T], F32, pc_)
                        acc_rot = Rot([acc_s[:, i] for i in range(2)], "acc_s")
                        sig_s = sb("sig_s", [128, 2, T], F32, pc_)
                        sig_rot = Rot([sig_s[:, i] for i in range(2)], "sig_s")
                        vst2 = sb("vst2", [128, 2, T], BF16, pc_)
                        vst2_rot = Rot([vst2[:, i] for i in range(2)], "vst2")
                        memset("pool", vext[:, :, :, 256:384], 1.0, (o_vext,))
                        for blk in range(16):
                            pb, po = psb[blk % 2], pso[blk % 2]
                            proj(("in", 32 + blk), 16, h_rhs, (o_hT,), pb, po)
                            cs_, cso = cst_rot.next()
                            ac, aco = acc_rot.next()
                            sg, sgo_ = sig_rot.next()
                            copy("pool", cs_[:, 0:3], mlhalo[:, blk, :], (o_mlhalo,), (cso,))
                            copy("act", cs_[:, 3:3 + T], pb[:, 0:T], (po,), (cso,))
                            w_after()

                            def cw(k, blk=blk):
                                i = l * 64 + k * 16 + blk
                                return cml[:, i:i + 1]

                            ts("dve", ac, cs_[:, 3:3 + T], cw(3), ALU.mult, (cso,), (aco,))
                            for k in range(3):
                                stt(ac, cs_[:, k:k + T], cw(k), ac, ALU.mult, ALU.add, (cso, aco), (aco,))
                            copy("pool", mlhalo[:, blk, :], cs_[:, T:T + 3], (cso,), (o_mlhalo,))
                            act(sg, ac, AF.Sigmoid, (aco,), (sgo_,))
                            stt(qm[:, blk, :], ac, (1.0 / 16.0) if blk < 8 else 1.0, sg, ALU.mult, ALU.mult,
                                (aco, sgo_), (o_qm,))
                        for cch in range(8):
                            pb, po = psb[cch % 2], pso[cch % 2]
                            proj(("in", 48 + cch), 16, h_rhs, (o_hT,), pb, po)
                            vs, vso = vst2_rot.next()
                            copy("dve", vs, pb[:, 0:T], (po,), (vso,))
                            w_after()
                            for sub in range(NSUB):
                                tr(psbf[:, sub * 128:(sub + 1) * 128], vs[:, sub * 128:(sub + 1) * 128], cb(IDF),
                                   (vso, o_cst), (pso[7],))
                            hd = cch // 2
                            off = (cch % 2) * 128
                            copy("act", vext[:, :, hd, off:off + 128],
                                 psbf[:, 0:512].rearrange("p (s c) -> p s c", s=NSUB), (pso[7],), (o_vext,))
                        for cch in range(8):
                            pb, po = psb[cch % 2], pso[cch % 2]
                            proj(("in", 56 + cch), 16, h_rhs, (o_hT,), pb, po)
                            act(sgo[:, cch, :], pb[:, 0:T], AF.Sigmoid, (po,), (o_sgo,))
                            w_after()
                        S_.fence(4)
                        pc_.close()
                        bc_s = sb("bc_s", [128, 2, 3, T], F32, pl_)
                        bc_objs = [Obj("bc0"), Obj("bc1")]
                        sm_s = sb("sm_s", [128, 2, 2, 6, 128], F32, pl_)
                        sm_rot = [Rot([sm_s[:, g, i] for i in range(2)], f"sm_s{g}") for g in range(2)]
                        qkw_s = sb("qkw_s", [128, 2, NSUB, 128], BF16, pl_)
                        qkw_rot = [Rot([qkw_s[:, g, i] for i in range(NSUB)], f"qkw_s{g}") for g in range(2)]
                        qt_s = sb("qt_s", [128, 2, NSUB, 2, 128], BF16, pl_)
                        qt_rot = [Rot([qt_s[:, g, i] for i in range(NSUB)], f"qt_s{g}") for g in range(2)]
                        kw_s = sb("kw_s", [128, 2, NSUB, 256], BF16, pl_)
                        kw_rot = [Rot([kw_s[:, g, i] for i in range(NSUB)], f"kw_s{g}") for g in range(2)]
                        ws_s = sb("ws_s", [128, 2, NSUB], F32, pl_)
                        ws_rot = [Rot([ws_s[:, g, i:i + 1] for i in range(NSUB)], f"ws_s{g}") for g in range(2)]
                        wsl, wo = w_take(("in", 112))
                        for kc in range(16):
                            mm(psb[0][0:4, 0:T], wsl[:, kc, 0:4], hT[:, kc, :], kc == 0, kc == 15, (wo, o_hT), (pso[0],))
                        for kc in range(16):
                            mm(psb[1][0:4, 0:T], wsl[:, kc, 4:8], hT[:, kc, :], kc == 0, kc == 15, (wo, o_hT), (pso[1],))
                        w_after()
                        R_I, R_F, R_B, R_G, R_M = range(5)
                        R_MS, R_WI, R_EM = R_I, R_F, R_B
                        rr = (o_rws,)
                        act(rws[:, R_I, :], psb[0][0:4, 0:T], AF.Identity, (pso[0], o_cst), rr, bias=igb[:, l:l + 1])
                        act(rws[:, R_F, :], psb[1][0:4, 0:T], AF.Identity, (pso[1], o_cst), rr, bias=fgb[:, l:l + 1])
                        act(rws[:, R_F, :], rws[:, R_F, :], AF.Exp, rr, rr, scale=-1.0)
                        act(rws[:, R_F, :], rws[:, R_F, :], AF.Ln, rr, rr, bias=1.0)
                        op("dve", lambda h: h.tensor_tensor_scan(out=rws[:, R_B, :], data0=ones_row[:], data1=rws[:, R_F, :],
                                                                 initial=bcar[:, 0:1], op0=ALU.mult, op1=ALU.add),
                           (o_rws, o_car, o_cst), rr)
                        tt("dve", rws[:, R_G, :], rws[:, R_I, :], rws[:, R_B, :], ALU.add, rr, rr)
                        copy("dve", bcar[:, 0:1], rws[:, R_B, T - 1:T], rr, (o_car,))
                        op("dve", lambda h: h.tensor_tensor_scan(out=rws[:, R_M, :], data0=ones_row[:], data1=rws[:, R_G, :],
                                                                 initial=mcar[:, 0:1], op0=ALU.mult, op1=ALU.max),
                           (o_rws, o_car, o_cst), rr)
                        for c in range(NSUB):
                            scal = mcar[:, 0:1] if c == 0 else rws[:, R_M, c * 128 - 1:c * 128]
                            ts("dve", rws[:, R_MS, c * 128:(c + 1) * 128], zero_row[:, 0:128], scal, ALU.add,
                               (o_rws, o_car, o_cst), rr)
                        copy("dve", mcar[:, 0:1], rws[:, R_M, T - 1:T], rr, (o_car,))
                        tt("dve", rws[:, R_WI, :], rws[:, R_MS, :], rws[:, R_M, :], ALU.subtract, rr, rr)
                        act(rws[:, R_WI, :], rws[:, R_WI, :], AF.Exp, rr, rr)
                        tt("dve", rws[:, R_EM, :], rws[:, R_B, :], rws[:, R_M, :], ALU.subtract, rr, rr)
                        act(rws[:, R_EM, :], rws[:, R_EM, :], AF.Exp, rr, rr)
                        for c in range(NSUB):
                            tr(psb[2][:, c * 4:(c + 1) * 4], rws[:, R_G, c * 128:(c + 1) * 128], cf(IDF)[0:4, 0:4],
                               (o_rws, o_cst), (pso[2],))
                        copy("dve", gcol[:], psb[2][:, 0:16], (pso[2],), (o_gcol,))
                        psbf2 = psb[2].bitcast(BF16)

                        def ml_gen(hd, g):
                            bc, bco = bc_s[:, g], bc_objs[g]
                            for qi, (rw, scl) in enumerate(((R_M, -1.0), (R_WI, 1.0), (R_EM, 1.0))):
                                pb, po = (psb[2], pso[2]) if qi % 2 == 0 else (psb[6], pso[6])
                                mm(pb[:, 0:T], sel_f[:, hd * 128:(hd + 1) * 128], rws[:, rw, :], True, True,
                                   (o_rws, o_cst), (po,))
                                act(bc[:, qi, :], pb[:, 0:T], AF.Copy, (po,), (bco,), scale=scl)
                            negM = bc[:, 0, :]
                            wib = bc[:, 1, :]
                            emb = bc[:, 2, :]
                            yield
                            pre = []
                            for c in range(NSUB):
                                cs = slice(c * 128, (c + 1) * 128)
                                gc = gcol[:, c * 4 + hd:c * 4 + hd + 1]
                                sm, smo = sm_rot[g].next()
                                wi_t, wim_t = sm[:, 0, :], sm[:, 1, :]
                                pQK, oQK = psb[5 + g][:, 0:128], pso[5 + g]
                                for kc in range(2):
                                    mm(pQK, qm[:, 8 + 2 * hd + kc, cs], qm[:, 2 * hd + kc, cs], kc == 0, kc == 1,
                                       (o_qm,), (oQK,))
                                act(wi_t, negM[:, cs], AF.Exp, (bco, o_gcol), (smo,), bias=gc)
                                tt("pool", wim_t, wi_t, cf(MLE), ALU.mult, (smo, o_cst), (smo,))
                                qkw, qkwo = qkw_rot[g].next()
                                tt("dve", qkw, pQK, wim_t, ALU.mult, (oQK, smo), (qkwo,))
                                qt, qto = qt_rot[g].next()
                                for kc in range(2):
                                    tt("pool", qt[:, kc, :], qm[:, 2 * hd + kc, cs], wib[:, cs], ALU.mult, (o_qm, bco),
                                       (qto,))
                                ws, wso = ws_rot[g].next()
                                act(ws, gc, AF.Exp, (o_gcol, bco), (wso,), bias=negM[:, c * 128 + 127:c * 128 + 128])
                                pKT, oKT = (psbf[:, 0:256], pso[7]) if g == 0 else (psbf2[:, 0:256], pso[2])
                                for kc in range(2):
                                    tr(pKT[:, kc * 128:(kc + 1) * 128], qm[:, 8 + 2 * hd + kc, cs], cb(IDF),
                                       (o_qm, o_cst), (oKT,))
                                kw, kwo = kw_rot[g].next()
                                act(kw, pKT, AF.Copy, (oKT, wso), (kwo,), scale=ws)
                                pre.append((qkw, qkwo, qt, qto, kw, kwo))
                                yield
                            for c in range(NSUB):
                                cs = slice(c * 128, (c + 1) * 128)
                                qkw, qkwo, qt, qto, kw, kwo = pre[c]
                                sm, smo = sm_rot[g].next()
                                dmx, rdn, hh0, hh1 = (sm[:, i, :] for i in range(2, 6))
                                pN, oN = psb[3 + g], pso[3 + g]
                                use_state = not (first_tile and c == 0)
                                for vch in range(3):
                                    vsl = slice(vch * 128, (vch + 1) * 128)
                                    mm(pN[:, vsl], vext[:, c, hd, vsl], qkw, True, not use_state, (o_vext, qkwo), (oN,))
                                    if use_state:
                                        for kc in range(2):
                                            mm(pN[:, vsl], cbf[:, hd, kc, vsl], qt[:, kc, :], False, kc == 1,
                                               (o_cbf[hd], qto), (oN,))
                                wprev = wib[:, c * 128 + 127:c * 128 + 128]
                                for kc in range(2):
                                    pC, oC = psb[kc], pso[kc]
                                    mm(pC[:, 0:384], kw[:, kc * 128:(kc + 1) * 128], vext[:, c, hd, :], True, True,
                                       (kwo, o_vext), (oC,))
                                    stt(cstate[:, hd, kc, :], cstate[:, hd, kc, :], wprev, pC[:, 0:384], ALU.mult, ALU.add,
                                        (o_cstate[hd], bco, oC), (o_cstate[hd],))
                                copy("act", cbf[:, hd], cstate[:, hd], (o_cstate[hd],), (o_cbf[hd],))
                                yield
                                act(dmx, pN[:, 256:384], AF.Abs, (oN,), (smo,))
                                tt("dve", dmx, dmx, emb[:, cs], ALU.max, (smo, bco), (smo,))
                                op("dve", lambda h: h.reciprocal(out=rdn, in_=dmx), (smo,), (smo,))
                                for vch, hb in ((0, hh0), (1, hh1)):
                                    vsl = slice(vch * 128, (vch + 1) * 128)
                                    tt("dve", hb, pN[:, vsl], rdn, ALU.mult, (oN, smo), (smo,))
                                    tt("pool", ybr[:, 2, 2 * hd + vch, cs], hb, sgo[:, 2 * hd + vch, cs], ALU.mult,
                                       (smo, o_sgo), (o_ybr[2],))
                                yield

                        for h0 in (0, 2):
                            run_rr([ml_gen(h0 + g, g) for g in range(2)])
                        S_.fence(5)

                    if dbg and ti == NT - 1:
                        with ExitStack() as pd_:
                            dbgt = sb("dbgt2", [128, 3, 8, T], F32, pd_)
                            o_dbgt = Obj("dbgt2")
                            copy("dve", dbgt[:], ybr[:], tuple(o_ybr), (o_dbgt,))
                            dma("pool", dbg_out["d_ypool"][:, :, :], dbgt[:, 0], (o_dbgt,), ())
                            dma("pool", dbg_out["d_ysb"][:, :, :], dbgt[:, 1], (o_dbgt,), ())
                            dma("pool", dbg_out["d_yml"][:, :, :], dbgt[:, 2], (o_dbgt,), ())
                            S_.fence(6)

                    merged = sb("merged", [128, 16, T], BF16, pm_)
                    o_merged = Obj("merged")
                    with ExitStack() as pf_:
                        sg_s = sb("sg_s", [128, 2, T], F32, pf_)
                        sg_rot = Rot([sg_s[:, i] for i in range(2)], "sg_s")
                        pr_s = sb("pr_s", [128, 3, T], F32, pf_)
                        pr_rot = Rot([pr_s[:, i] for i in range(3)], "pr_s")
                        mac_s = sb("mac_s", [128, 2, T], F32, pf_)
                        mac_rot = Rot([mac_s[:, i] for i in range(2)], "mac_s")
                        bk = 0
                        for d in range(16):
                            prods = []
                            for br in range(3):
                                pG, oG = psb[bk % 4], pso[bk % 4]
                                pP, oP = psb[4 + bk % 4], pso[4 + bk % 4]
                                bk += 1
                                proj(("gate", br, d), 16, h_rhs, (o_hT,), pG, oG)
                                w_after()
                                proj(("br", br, d), 8, lambda kc, br=br: ybr[:, br, kc, :], (o_ybr[br],), pP, oP)
                                w_after()
                                sg, sgo_ = sg_rot.next()
                                act(sg, pG[:, 0:T], AF.Sigmoid, (oG,), (sgo_,))
                                pr, pro = pr_rot.next()
                                tt("dve", pr, pP[:, 0:T], sg, ALU.mult, (oP, sgo_), (pro,))
                                prods.append((pr, pro))
                            mac, maco = mac_rot.next()
                            tt("pool", mac, prods[0][0], prods[1][0], ALU.add, (prods[0][1], prods[1][1]), (maco,))
                            tt("pool", merged[:, d, :], mac, prods[2][0], ALU.add, (maco, prods[2][1]), (o_merged,))
                        S_.fence(7)
                    if dbg and ti == NT - 1:
                        with ExitStack() as pd_:
                            dbgt = sb("dbgt3", [128, 16, T], F32, pd_)
                            o_dbgt = Obj("dbgt3")
                            copy("dve", dbgt[:], merged[:], (o_merged,), (o_dbgt,))
                            dma("pool", dbg_out["d_merged"][:, :, :], dbgt[:], (o_dbgt,), ())
                            S_.fence(8)

                    def residual_ln(u, o_u, lidx, gseg, final_fn):
                        with ExitStack() as pn_:
                            ub_s = sb("ub_s", [128, 2, T], BF16, pn_)
                            ub_rot = Rot([ub_s[:, i] for i in range(2)], "ub_s")
                            sq_s = sb("sq_s", [128, 2, T], BF16, pn_)
                            sq_rot = Rot([sq_s[:, i] for i in range(2)], "sq_s")
                            st_s = sb("st_s", [128, 4, T], F32, pn_)
                            o_st = Obj("st_s")
                            pS1, oS1 = psb[4], pso[4]
                            pS2, oS2 = psb[5], pso[5]
                            for d in range(16):
                                ub, ubo = ub_rot.next()
                                sq, sqo = sq_rot.next()
                                copy("dve", ub, u[:, d, :], (o_u[d],), (ubo,))
                                act(sq, u[:, d, :], AF.Square, (o_u[d],), (sqo,))
                                mm(pS1[:, 0:T], cb(OND), ub, d == 0, d == 15, (ubo, o_cst), (oS1,))
                                mm(pS2[:, 0:T], cb(OND), sq, d == 0, d == 15, (sqo, o_cst), (oS2,))
                            mean = st_s[:, 0, :]
                            msq = st_s[:, 1, :]
                            var = st_s[:, 2, :]
                            rstd = st_s[:, 3, :]
                            act(mean, pS1[:, 0:T], AF.Copy, (oS1,), (o_st,))
                            act(msq, pS1[:, 0:T], AF.Square, (oS1,), (o_st,))
                            tt("dve", var, pS2[:, 0:T], msq, ALU.subtract, (oS2, o_st), (o_st,))
                            ts("dve", var, var, 0.0, ALU.max, (o_st,), (o_st,))
                            act(var, var, AF.Ln, (o_st,), (o_st,), bias=LN_EPS)
                            act(rstd, var, AF.Exp, (o_st,), (o_st,), scale=-0.5)
                            stt(msq, mean, -1.0, rstd, ALU.mult, ALU.mult, (o_st,), (o_st,))
                            for d in range(16):
                                tt("dve", u[:, d, :], u[:, d, :], rstd, ALU.mult, (o_u[d], o_st), (o_u[d],))
                                tt("dve", u[:, d, :], u[:, d, :], msq, ALU.add, (o_u[d], o_st), (o_u[d],))
                                final_fn(d)
                            S_.fence(9)

                    with ExitStack() as pg_:
                        u = sb("u", [128, 16, T], F32, pg_)
                        o_u = [Obj(f"u{d}") for d in range(16)]
                        x1st = sb("x1st", [128, 2, T], F32, pg_)
                        x1_rot = Rot([x1st[:, i] for i in range(2)], "x1st")
                        for q4 in range(4):
                            dma("pool", u[:, q4 * 4:(q4 + 1) * 4, :], xs_d[src_j, b, :, q4 * 4:(q4 + 1) * 4, t0:t0 + T],
                                (xs_obj[src_j][b][ti],), tuple(o_u[q4 * 4:(q4 + 1) * 4]))
                        for d in range(16):
                            act(u[:, d, :], u[:, d, :], AF.Copy, (o_u[d],), (o_u[d],), scale=ALPHA)
                        for d in range(16):
                            pb, po = psb[d % 4], pso[d % 4]
                            proj(("out", d), 16, lambda kc: merged[:, kc, :], (o_merged,), pb, po)
                            w_after()
                            stt(u[:, d, :], pb[:, 0:T], mcol(l, b, 2, d), u[:, d, :], ALU.mult, ALU.add,
                                (po, o_mod, o_u[d]), (o_u[d],))

                        def fin1(d):
                            xs1, xs1o = x1_rot.next()
                            act(xs1, u[:, d, :], AF.Identity, (o_u[d], o_cst), (xs1o,),
                                bias=lnb[:, l * 32 + d:l * 32 + d + 1], scale=lng[:, l * 32 + d:l * 32 + d + 1])
                            dma("act", x1_d[:, d, :], xs1, (xs1o,), (x1_obj,))
                            if dbg and ti == NT - 1:
                                dma("pool", dbg_out["d_x1"][:, d, :], xs1, (xs1o,), ())
                            act(hT[:, d, :], u[:, d, :], AF.Identity, (o_u[d], o_a2), (o_hT,),
                                bias=a2t[:, 16 + d:17 + d], scale=a2t[:, d:d + 1])

                        residual_ln(u, o_u, l, 2, fin1)
                S_.fence(10)

                with ExitStack() as pf2_:
                    actT = sb("actT", [128, 44, T], BF16, pf2_)
                    o_actT = Obj("actT")
                    with ExitStack() as ph_:
                        ust = sb("ust", [128, 4, 2 + T], F32, ph_)
                        ust_rot = Rot([ust[:, i] for i in range(4)], "ust")
                        fac = sb("fac", [128, 4, T], F32, ph_)
                        fac_rot = Rot([fac[:, i] for i in range(4)], "fac")
                        fsg = sb("fsg", [128, 2, T], F32, ph_)
                        fsg_rot = Rot([fsg[:, i] for i in range(2)], "fsg")
                        bk = 0
                        for p in range(44):
                            accs = []
                            for which in range(2):
                                blk = p + 44 * which
                                pb, po = psb[bk % 4], pso[bk % 4]
                                bk += 1
                                proj(("up", blk), 16, h_rhs, (o_hT,), pb, po)
                                w_after()
                                us, uso = ust_rot.next()
                                ac, aco = fac_rot.next()
                                copy("pool", us[:, 0:2], ffhalo[:, blk, :], (o_ffhalo,), (uso,))
                                copy("act", us[:, 2:2 + T], pb[:, 0:T], (po,), (uso,))

                                def fw(k, blk=blk):
                                    i = l * 264 + k * 88 + blk
                                    return cff[:, i:i + 1]

                                act(ac, pb[:, 0:T], AF.Copy, (po, o_cst), (aco,), scale=fw(2))
                                stt(ac, us[:, 1:1 + T], fw(1), ac, ALU.mult, ALU.add, (uso, aco), (aco,))
                                stt(ac, us[:, 0:T], fw(0), ac, ALU.mult, ALU.add, (uso, aco), (aco,))
                                copy("pool", ffhalo[:, blk, :], us[:, T:T + 2], (uso,), (o_ffhalo,))
                                accs.append((ac, aco))
                            sg, sgo_ = fsg_rot.next()
                            act(sg, accs[1][0], AF.Silu, (accs[1][1],), (sgo_,))
                            tt("dve", actT[:, p, :], sg, accs[0][0], ALU.mult, (sgo_, accs[0][1]), (o_actT,))
                        S_.fence(11)

                    with ExitStack() as pi_:
                        u = sb("u2", [128, 16, T], F32, pi_)
                        o_u = [Obj(f"u2_{d}") for d in range(16)]
                        x2st = sb("x2st", [128, 2, T], F32, pi_)
                        x2_rot = Rot([x2st[:, i] for i in range(2)], "x2st")
                        ost = sb("ost", [128, 2, 512], F32, pi_)
                        ost_rot = Rot([ost[:, i] for i in range(2)], "ost")
                        for q4 in range(4):
                            dma("pool", u[:, q4 * 4:(q4 + 1) * 4, :], x1_d[:, q4 * 4:(q4 + 1) * 4, :], (x1_obj,),
                                tuple(o_u[q4 * 4:(q4 + 1) * 4]))
                        for d in range(16):
                            act(u[:, d, :], u[:, d, :], AF.Copy, (o_u[d],), (o_u[d],), scale=ALPHA)
                        for d in range(16):
                            pb, po = psb[d % 4], pso[d % 4]
                            for pc in range(4):
                                wsl, wo = w_take(("dn", d, pc))
                                for kc in range(11):
                                    mm(pb[:, 0:T], wsl[:, kc, :], actT[:, pc * 11 + kc, :], pc == 0 and kc == 0,
                                       pc == 3 and kc == 10, (wo, o_actT), (po,))
                                w_after()
                            stt(u[:, d, :], pb[:, 0:T], mcol(l, b, 5, d), u[:, d, :], ALU.mult, ALU.add,
                                (po, o_mod, o_u[d]), (o_u[d],))

                        def fin2(d):
                            gcol_ = lng[:, l * 32 + 16 + d:l * 32 + 17 + d]
                            bcol_ = lnb[:, l * 32 + 16 + d:l * 32 + 17 + d]
                            if not last_layer:
                                xs2, xs2o = x2_rot.next()
                                act(xs2, u[:, d, :], AF.Identity, (o_u[d], o_cst), (xs2o,), bias=bcol_, scale=gcol_)
                                dma("act", xs_d[dst_j, b, :, d, t0:t0 + T], xs2, (xs2o,), (xs_obj[dst_j][b][ti],))
                            else:
                                act(u[:, d, :], u[:, d, :], AF.Identity, (o_u[d], o_cst), (o_u[d],), bias=bcol_,
                                    scale=gcol_)

                        residual_ln(u, o_u, l, 5, fin2)
                        if last_layer:
                            for sub in range(NSUB):
                                for g4 in range(4):
                                    pb, po = psb[g4 % 4], pso[g4 % 4]
                                    for j in range(4):
                                        d = g4 * 4 + j
                                        tr(pb[:, j * 128:(j + 1) * 128], u[:, d, sub * 128:(sub + 1) * 128], cf(IDF),
                                           (o_u[d], o_cst), (po,))
                                    os_, oso = ost_rot.next()
                                    copy("act" if g4 % 2 == 0 else "dve", os_, pb[:, 0:512], (po,), (oso,))
                                    r0 = b * S + t0 + sub * 128
                                    dma("pool", out_d[r0:r0 + 128, g4 * 512:(g4 + 1) * 512], os_, (oso,), ())
                        S_.fence(12)
                S_.fence(13)
    S_.barrier(engines=("sp", "pool"))
    es.close()
    return nc, S_.ninst


def host_consts():
    c = np.zeros((128, 9, 128), np.float32)
    i = np.arange(128)
    c[:, 0, :] = np.eye(128)
    c[:, 1, :] = (i[:, None] >= i[None, :])
    c[:, 2, :] = (i[:, None] < i[None, :])
    c[:, 3, :] = (i[:, None] < i[None, :])
    c[:, 4, :] = (i[:, None] <= i[None, :])
    c[:, 5, :] = 1.0
    c[:, 6, :] = 1.0 / D
    sel = np.zeros((4, 4, 128), np.float32)
    for h in range(4):
        sel[h, h, :] = 1.0
    rc = np.zeros((128, 4, 16), np.float32)
    for g, w in enumerate(POOL_W):
        t = np.arange(16)
        rc[:, g, :] = (w / np.minimum(t + 1, w))[None, :]
    return c, sel.reshape(4, 512), rc.reshape(128, 64)


def make_in_maps(inputs, NSEQ, S, L, ncores):
    f = lambda a: np.ascontiguousarray(np.asarray(a, dtype=np.float32))
    cst, sel, rc = host_consts()
    shared = {
        "w_ada": f(inputs["w_ada"]),
        "b_ada": f(inputs["b_ada"]).reshape(L * 96, 128),
        "w_in": f(inputs["w_in"]),
        "conv_ml": f(inputs["conv_ml"]).reshape(L * 64, 128),
        "pool_w": f(inputs["pool_w"]),
        "pool_scale": f(inputs["pool_scale"]).reshape(L * 8, 128),
        "ig_bias": f(inputs["ig_bias"]).reshape(L, 4, 1),
        "fg_bias": f(inputs["fg_bias"]).reshape(L, 4, 1),
        "w_branch": f(inputs["w_branch"]),
        "w_out": f(inputs["w_out"]),
        "w_up": f(inputs["w_up"]),
        "conv_ff": f(inputs["conv_ff"]).reshape(L * 264, 128),
        "w_down": f(inputs["w_down"]),
        "ln_g": f(inputs["ln_g"]).reshape(L * 32, 128),
        "ln_b": f(inputs["ln_b"]).reshape(L * 32, 128),
        "cst": cst, "sel": sel, "rcnt": rc,
    }
    x = f(inputs["x"])
    c = f(inputs["c"])
    maps = []
    for i in range(ncores):
        m = dict(shared)
        m["x"] = np.ascontiguousarray(x[i * NSEQ:(i + 1) * NSEQ].reshape(NSEQ * S, D))
        m["c"] = np.ascontiguousarray(c[i * NSEQ:(i + 1) * NSEQ].reshape(NSEQ * KC, 128))
        maps.append(m)
    return maps


def run(inputs, ncores, dbg=False):
    x = np.asarray(inputs["x"])
    B, S, _ = x.shape
    L = np.asarray(inputs["w_in"]).shape[0]
    NSEQ = B // ncores
    nc, ninst = build_program(NSEQ, S, L, dbg=dbg)
    maps = make_in_maps(inputs, NSEQ, S, L, ncores)
    res = run_bass_kernel_spmd(nc, maps, core_ids=list(range(ncores)))
    out = np.concatenate([np.asarray(r["out"]).reshape(NSEQ, S, D) for r in res.results], axis=0)
    if dbg:
        return out.astype(np.float32), res.results
    return out.astype(np.float32)


def kernel(**inputs):
    return run(inputs, NCORES)
```
